# Optimizing a Trainium2 kernel written in Bass

```python
import jax, jax.numpy as jnp
from jax import lax
import numpy as np

D_MODEL = 1024
BATCH = 16
SEQ = 2048
DEPTH = 2
DEC_BATCH = 2
DEC_SEQ = 8192
PAST_LEN = 128

N_MIXERS = 2
N_POOL_LAYERS = (DEPTH + 1) // 2
N_ATTN_LAYERS = DEPTH // 2
EPS = 1e-6
POOL_WINDOWS = (2, 4, 8, 16)
N_GROUPS = len(POOL_WINDOWS)
GROUP = D_MODEL // N_GROUPS
HEAD_DIM = 64
N_HEADS = D_MODEL // HEAD_DIM
N_KV = 4
GQA_GROUP = N_HEADS // N_KV
WINDOW = 128
BLOCK = 128
KEYS = BLOCK + 2 * WINDOW
ROPE_THETA = 10000.0
N_MEM = 256
X_HEADS = 4
X_HEAD_DIM = D_MODEL // X_HEADS
D_FF = 4 * D_MODEL

kernel_name = "hybrid_pool_window_gqa_memory_encoder"


def rmsnorm(x, g):
    xf = x.astype(jnp.float32)
    y = xf * lax.rsqrt(jnp.mean(xf * xf, axis=-1, keepdims=True) + EPS)
    return (y * g.astype(jnp.float32)).astype(x.dtype)


def rope_tables(seq_len):
    inv_freq = ROPE_THETA ** (-jnp.arange(0, HEAD_DIM, 2, dtype=jnp.float32) / HEAD_DIM)
    ang = jnp.arange(seq_len, dtype=jnp.float32)[:, None] * inv_freq[None, :]
    return jnp.cos(ang), jnp.sin(ang)


def apply_rope(x, cos, sin):
    xf = x.astype(jnp.float32)
    x1, x2 = jnp.split(xf, 2, axis=-1)
    c = cos[None, :, None, :]
    s = sin[None, :, None, :]
    return jnp.concatenate([x1 * c - x2 * s, x2 * c + x1 * s], axis=-1).astype(x.dtype)


def pool_mixer(h, w_pool, scale):
    B, S, D = h.shape
    hf = h.astype(jnp.float32)
    cs = jnp.concatenate([jnp.zeros((B, 1, D), jnp.float32), jnp.cumsum(hf, axis=1)], axis=1)
    t = jnp.arange(S)
    pooled = []
    for g, w in enumerate(POOL_WINDOWS):
        lo = jnp.clip(t - w // 2, 0, S)
        hi = jnp.clip(t + w // 2, 0, S)
        count = (hi - lo).astype(jnp.float32)
        seg = cs[:, :, g * GROUP:(g + 1) * GROUP]
        total = jnp.take(seg, hi, axis=1) - jnp.take(seg, lo, axis=1)
        pooled.append(total / count[None, :, None])
    diff = jnp.concatenate(pooled, axis=-1) - hf
    y = jnp.einsum("bsgc,gcd->bsgd", diff.reshape(B, S, N_GROUPS, GROUP), w_pool.astype(jnp.float32))
    y = y.reshape(B, S, D) * scale.astype(jnp.float32)
    return y.astype(h.dtype)


def window_attention(h, w_qkv, w_o, sinks, cos, sin):
    B, S, _ = h.shape
    nb = S // BLOCK
    qkv = h @ w_qkv
    q = qkv[..., :N_HEADS * HEAD_DIM].reshape(B, S, N_HEADS, HEAD_DIM)
    k = qkv[..., N_HEADS * HEAD_DIM:(N_HEADS + N_KV) * HEAD_DIM].reshape(B, S, N_KV, HEAD_DIM)
    v = qkv[..., (N_HEADS + N_KV) * HEAD_DIM:].reshape(B, S, N_KV, HEAD_DIM)
    q = apply_rope(q, cos, sin) * (HEAD_DIM ** -0.5)
    k = apply_rope(k, cos, sin)
    pad = ((0, 0), (WINDOW, WINDOW), (0, 0), (0, 0))
    kp = jnp.pad(k, pad)
    vp = jnp.pad(v, pad)
    qb = q.reshape(B, nb, BLOCK, N_KV, GQA_GROUP, HEAD_DIM).transpose(1, 0, 2, 3, 4, 5)
    r = jnp.arange(BLOCK)
    c = jnp.arange(KEYS)
    sink = sinks.astype(jnp.float32).reshape(N_KV, GQA_GROUP)

    def block(args):
        i, qi = args
        ki = lax.dynamic_slice_in_dim(kp, i * BLOCK, KEYS, axis=1)
        vi = lax.dynamic_slice_in_dim(vp, i * BLOCK, KEYS, axis=1)
        s = jnp.einsum("bqkgd,bskd->bkgqs", qi, ki).astype(jnp.float32)
        qpos = i * BLOCK + r
        kpos = i * BLOCK - WINDOW + c
        mask = (jnp.abs(kpos[None, :] - qpos[:, None]) <= WINDOW) & (kpos[None, :] >= 0) & (kpos[None, :] < S)
        s = jnp.where(mask[None, None, None], s, -1e30)
        sink_col = jnp.broadcast_to(sink[None, :, :, None, None], s.shape[:-1] + (1,))
        p = jax.nn.softmax(jnp.concatenate([s, sink_col], axis=-1), axis=-1)[..., :-1]
        o = jnp.einsum("bkgqs,bskd->bqkgd", p.astype(vi.dtype), vi)
        return o.reshape(B, BLOCK, N_HEADS * HEAD_DIM)

    o = lax.map(block, (jnp.arange(nb), qb))
    o = o.transpose(1, 0, 2, 3).reshape(B, S, N_HEADS * HEAD_DIM)
    return o @ w_o


def memory_attention(h, mem_n, w_q, w_kv, w_o):
    B, S, _ = h.shape
    q = (h @ w_q).reshape(B, S, X_HEADS, X_HEAD_DIM) * (X_HEAD_DIM ** -0.5)
    kv = mem_n @ w_kv
    k = kv[..., :D_MODEL].reshape(B, N_MEM, X_HEADS, X_HEAD_DIM)
    v = kv[..., D_MODEL:].reshape(B, N_MEM, X_HEADS, X_HEAD_DIM)
    s = jnp.einsum("bshd,bmhd->bhsm", q, k).astype(jnp.float32)
    p = jax.nn.softmax(s, axis=-1).astype(v.dtype)
    o = jnp.einsum("bhsm,bmhd->bshd", p, v).reshape(B, S, D_MODEL)
    return o @ w_o


def sq_relu_mlp(h, w_up, w_down):
    u = jax.nn.relu(h @ w_up)
    return (u * u) @ w_down


def trunk(x, mem, norm_mix, pool_w, pool_scale, attn_qkv, attn_o, attn_sink,
          norm_x, norm_mem, x_wq, x_wkv, x_wo, norm_mlp, w_up, w_down, norm_final):
    S = x.shape[1]
    cos, sin = rope_tables(S)
    for i in range(DEPTH):
        j = i // N_MIXERS
        hn = rmsnorm(x, norm_mix[i])
        if i % N_MIXERS == 0:
            x = x + pool_mixer(hn, pool_w[j], pool_scale[j])
        else:
            x = x + window_attention(hn, attn_qkv[j], attn_o[j], attn_sink[j], cos, sin)
        x = x + memory_attention(rmsnorm(x, norm_x[i]), rmsnorm(mem, norm_mem[i]), x_wq[i], x_wkv[i], x_wo[i])
        x = x + sq_relu_mlp(rmsnorm(x, norm_mlp[i]), w_up[i], w_down[i])
    return rmsnorm(x, norm_final)


def setup_inputs(seed: int = 0) -> dict:
    key = jax.random.key(seed)
    ks = jax.random.split(key, 24)
    f32 = jnp.float32

    def nrm(k, shape, scale):
        return jax.random.normal(k, shape, f32) * scale

    def gain(k, shape):
        return 1.0 + 0.05 * jax.random.normal(k, shape, f32)

    D = D_MODEL
    return {
        "x_prompt": nrm(ks[0], (BATCH, SEQ, D), 1.0),
        "x_sample": nrm(ks[1], (DEC_BATCH, DEC_SEQ, D), 1.0),
        "mem_prompt": nrm(ks[2], (BATCH, N_MEM, D), 1.0),
        "mem_sample": nrm(ks[3], (DEC_BATCH, N_MEM, D), 1.0),
        "norm_mix": gain(ks[4], (DEPTH, D)),
        "pool_w": nrm(ks[5], (N_POOL_LAYERS, N_GROUPS, GROUP, GROUP), GROUP ** -0.5),
        "pool_scale": 0.5 + 0.1 * jax.random.normal(ks[6], (N_POOL_LAYERS, D), f32),
        "attn_qkv": nrm(ks[7], (N_ATTN_LAYERS, D, (N_HEADS + 2 * N_KV) * HEAD_DIM), D ** -0.5),
        "attn_o": nrm(ks[8], (N_ATTN_LAYERS, N_HEADS * HEAD_DIM, D), (N_HEADS * HEAD_DIM) ** -0.5),
        "attn_sink": nrm(ks[9], (N_ATTN_LAYERS, N_HEADS), 0.5),
        "norm_x": gain(ks[10], (DEPTH, D)),
        "norm_mem": gain(ks[11], (DEPTH, D)),
        "x_wq": nrm(ks[12], (DEPTH, D, D), D ** -0.5),
        "x_wkv": nrm(ks[13], (DEPTH, D, 2 * D), D ** -0.5),
        "x_wo": nrm(ks[14], (DEPTH, D, D), D ** -0.5),
        "norm_mlp": gain(ks[15], (DEPTH, D)),
        "w_up": nrm(ks[16], (DEPTH, D, D_FF), D ** -0.5),
        "w_down": nrm(ks[17], (DEPTH, D_FF, D), D_FF ** -0.5),
        "norm_final": gain(ks[18], (D,)),
    }


def reference(x_prompt, x_sample, mem_prompt, mem_sample, norm_mix, pool_w, pool_scale,
              attn_qkv, attn_o, attn_sink, norm_x, norm_mem, x_wq, x_wkv, x_wo,
              norm_mlp, w_up, w_down, norm_final):
    y_prompt = trunk(x_prompt, mem_prompt, norm_mix, pool_w, pool_scale, attn_qkv, attn_o, attn_sink,
                     norm_x, norm_mem, x_wq, x_wkv, x_wo, norm_mlp, w_up, w_down, norm_final)
    y_sample = trunk(x_sample, mem_sample, norm_mix, pool_w, pool_scale, attn_qkv, attn_o, attn_sink,
                     norm_x, norm_mem, x_wq, x_wkv, x_wo, norm_mlp, w_up, w_down, norm_final)
    return (y_prompt, y_sample)
```

```python
import numpy as np
from contextlib import ExitStack
import concourse.bass as bass
import concourse.mybir as mybir
from concourse.bass_utils import run_bass_kernel_spmd

F32 = mybir.dt.float32
BF16 = mybir.dt.bfloat16
AF = mybir.ActivationFunctionType
ALU = mybir.AluOpType

D = 1024
KC = 8
EPS = 1e-6
NCORES = 8
SEQ = 2048
DEC_SEQ = 8192
NSLOT = 6
SLOT_ELEMS = 4096
NPB = 28
DBG = {}
EMBED_WAITS = True
ARENA_EL = 24576


class Buf:
    __slots__ = ("name", "w", "r", "sem", "semcnt", "uid", "excl")
    _next = [0]

    def __init__(self, name):
        Buf._next[0] += 1
        self.uid = Buf._next[0]
        self.excl = False
        self.name = name
        self.w = None
        self.r = {}
        self.sem = None
        self.semcnt = 0


class Sched:
    ENGS = ("pe", "act", "dve", "pool", "sp")
    COMPUTE = ("pe", "act", "dve", "pool")

    def __init__(self, nc, stack):
        self.nc = nc
        self.stack = stack
        self.ops = {e: [] for e in self.ENGS}
        self.cnt = {e: 0 for e in self.ENGS}
        self.waited = {e: {} for e in self.ENGS}
        self.sems = {}
        for e in self.COMPUTE:
            self.sems[e] = stack.enter_context(nc.semaphore("sem_" + e))
        self.nbuf = 0
        self.store_events = []

    def buf(self, name):
        return Buf(name)

    def _dma_sem(self, b):
        if b.sem is None:
            self.nbuf += 1
            b.sem = self.stack.enter_context(self.nc.semaphore("dsem%d" % self.nbuf))
            self.sems[("dma", b.uid)] = b.sem
        return ("dma", b.uid)

    def _wait(self, eng, deps):
        for (key, v) in deps:
            if key == eng and eng == "pe":
                continue
            if self.waited[eng].get(key, 0) >= v:
                continue
            self.ops[eng].append(("wait", key, v))
            self.waited[eng][key] = v

    @staticmethod
    def _deps(reads, writes):
        deps = []
        for b in reads:
            if b.w is not None:
                deps.append(b.w)
            if b.excl:
                deps.extend(b.r.items())
        for b in writes:
            if b.w is not None:
                deps.append(b.w)
            deps.extend(b.r.items())
        return deps

    @staticmethod
    def _mark(ev, reads, writes):
        k, v = ev
        for b in reads:
            if b.r.get(k, 0) < v:
                b.r[k] = v
        for b in writes:
            b.w = ev
            b.r = {}

    def emit(self, eng, fn, reads=(), writes=(), inc=True):
        self._wait(eng, self._deps(reads, writes))
        if inc:
            self.cnt[eng] += 1
            ev = (eng, self.cnt[eng])
        else:
            assert eng == "pe"
            ev = (eng, self.cnt[eng] + 1)
        self.ops[eng].append(("op", fn, inc))
        self._mark(ev, reads, writes)
        return ev

    def dma(self, eng, out_ap, in_ap, b, load):
        key = self._dma_sem(b)
        reads = [] if load else [b]
        writes = [b] if load else []
        self._wait(eng, self._deps(reads, writes))
        b.semcnt += 16
        ev = (key, b.semcnt)
        self.ops[eng].append(("dma", out_ap, in_ap, b.sem))
        self._mark(ev, reads, writes)
        if not load:
            self.store_events.append(ev)
        return ev

    def barrier(self):
        evs = [(e, self.cnt[e]) for e in ("pe", "act", "dve") if self.cnt[e] > 0]
        for e in ("pe", "act", "dve", "sp"):
            self._wait(e, [x for x in evs if x[0] != e])

    def finish(self):
        self._wait("sp", self.store_events)

    def replay(self, block):
        sems = self.sems

        def run(engname, e):
            pending = []
            for op in self.ops[engname]:
                if op[0] == "wait":
                    pending.append(op)
                elif op[0] == "op":
                    emb = pending.pop() if (pending and EMBED_WAITS) else None
                    for w in pending:
                        e.wait_ge(sems[w[1]], w[2])
                    pending = []
                    ins = op[1](e)
                    if emb is not None:
                        ins._wait_ge(sems[emb[1]], emb[2])
                    if op[2]:
                        ins.then_inc(sems[engname], 1)
                else:
                    for w in pending:
                        e.wait_ge(sems[w[1]], w[2])
                    pending = []
                    e.dma_start(out=op[1], in_=op[2]).then_inc(op[3], 16)
            for w in pending:
                e.wait_ge(sems[w[1]], w[2])

        @block.sync
        def _(e):
            run("sp", e)

        @block.tensor
        def _(e):
            run("pe", e)

        @block.scalar
        def _(e):
            run("act", e)

        @block.vector
        def _(e):
            run("dve", e)

        @block.gpsimd
        def _(e):
            run("pool", e)


def build_program(stop_after=None, segs_sel=(0, 1, 2)):
    nc = bass.Bass("TRN2", target_bir_lowering=False)

    def din(name, shape):
        return nc.dram_tensor(name, list(shape), F32, kind="ExternalInput").ap()

    xp = din("xp", [2, SEQ, D])
    xs = din("xs", [2560, D])
    memp = din("memp", [2, 256, D])
    mems = din("mems", [256, D])
    pool_w = din("pool_w", [1024, 256])
    pool_scale = din("pool_scale", [1, D])
    attn_qkv = din("attn_qkv", [D, 1536])
    attn_o = din("attn_o", [D, D])
    attn_sink = din("attn_sink", [1, 16])
    x_wq = din("x_wq", [2 * D, D])
    x_wkv = din("x_wkv", [2 * D, 2 * D])
    x_wo = din("x_wo", [2 * D, D])
    w_up = din("w_up", [2 * D, 4 * D])
    w_down = din("w_down", [8 * D, D])
    gfm_d = din("gfm", [128, 72])
    gfinal_d = din("gfinal", [1, D])
    ident_d = din("ident", [128, 128])
    pb_d = din("pbc", [128, NPB * 128])
    masks_d = din("masks", [128, 4 * 128])
    ropeP_d = din("ropeP", [128, 16 * 64])
    ropeS_d = din("ropeS", [128, 18 * 64])
    yp = nc.dram_tensor("yp", [2, SEQ, D], F32, kind="ExternalOutput").ap()
    ys = nc.dram_tensor("ys", [SEQ, D], F32, kind="ExternalOutput").ap()

    with ExitStack() as st:
        S = Sched(nc, st)

        arena_state = [0]

        def sb(name, shape, dt, stack=st):
            if stack is st:
                return st.enter_context(nc.sbuf_tensor("s_" + name, list(shape), dt))
            n = 1
            for d in shape[1:]:
                n *= d
            el = 2 * n if dt == F32 else n
            off = (arena_state[0] + 31) // 32 * 32
            assert off + el <= ARENA_EL, (name, off, el)
            arena_state[0] = off + el
            ap = arena[0:shape[0], off:off + el]
            if dt == F32:
                ap = ap.bitcast(F32)
            if len(shape) > 2:
                names = ["d%d" % i for i in range(len(shape) - 1)]
                kw = {nm: d for nm, d in zip(names, shape[1:])}
                ap = ap.rearrange("p (%s) -> p %s" % (" ".join(names), " ".join(names)), **kw)
            return ap

        class phase:
            def __enter__(self):
                arena_state[0] = 0
                return "arena"

            def __exit__(self, *a):
                return False

        arena = st.enter_context(nc.sbuf_tensor("arena", [128, ARENA_EL], BF16))
        X = sb("X", [128, 18, D], F32)
        Xb = [S.buf("X%d" % i) for i in range(18)]
        ring = sb("ring", [128, NSLOT, SLOT_ELEMS], BF16)
        ringb = [S.buf("ring%d" % i) for i in range(NSLOT)]
        memKT = sb("memKT", [128, 8, 256], BF16)
        memKTb = S.buf("memKT")
        memV = sb("memV", [128, 2, D], BF16)
        memVb = S.buf("memV")
        Wp = sb("Wp", [128, 4, 2, 256], BF16)
        Wpb = S.buf("Wp")
        PB = sb("PB", [128, NPB, 128], BF16)
        PBb = S.buf("PB")
        gfin = sb("gfin", [128, D], F32)
        gfinb = S.buf("gfin")
        ident = sb("ident", [128, 128], BF16)
        identb = S.buf("ident")
        masks = sb("masks", [128, 4, 128], BF16)
        masksb = S.buf("masks")
        gfm = sb("gfm", [128, 72], F32)
        gfmb = S.buf("gfm")
        junk = sb("junk", [128, D], BF16)
        junkb = S.buf("junk")
        hn = sb("hn", [128, 4, D], BF16)
        hnb = [S.buf("hn%d" % i) for i in range(4)]
        ss = sb("ss", [128, 2, 4], F32)
        lnv = sb("lnv", [128, 2, 4], F32)
        rstd = sb("rstd", [128, 2, 4], F32)
        statb = [S.buf("stat0"), S.buf("stat1")]
        onesb16 = sb("ones", [128, 128], BF16)
        onesb = S.buf("ones")
        sinkL = sb("sinkL", [1, 128], BF16)
        sinkE = sb("sinkE", [1, 16], F32)
        sinkEb = sb("sinkEb", [128, 16], F32)
        sinkEbb = S.buf("sinkEb")
        sinkb = S.buf("sink")

        banks = [st.enter_context(nc.psum_tensor("bank%d" % i, [128, 512], F32)) for i in range(8)]
        bankb = [S.buf("bank%d" % i) for i in range(8)]
        for b_ in bankb:
            b_.excl = True
        bank_rr = [0]

        def psum():
            i = bank_rr[0]
            bank_rr[0] = (i + 1) % 8
            return banks[i], bankb[i]

        stat_rr = [0]

        def wrows(w, r0, nrow_chunks, c0, ncols):
            return w[r0:r0 + nrow_chunks * 128, c0:c0 + ncols].rearrange("(k p) n -> p k n", p=128)

        def wdesc(tag):
            kind = tag[0]
            if kind == "qkv":
                return wrows(attn_qkv, 0, 8, tag[1] * 512, 512), 8, 512
            if kind == "ao":
                return wrows(attn_o, 0, 8, tag[1] * 512, 512), 8, 512
            if kind == "kv":
                return wrows(x_wkv, tag[1] * D, 8, tag[2] * 512, 512), 8, 512
            if kind == "wq":
                return wrows(x_wq, tag[1] * D, 8, tag[2] * 512, 512), 8, 512
            if kind == "wo":
                return wrows(x_wo, tag[1] * D, 8, tag[2] * 512, 512), 8, 512
            if kind == "up":
                return wrows(w_up, tag[1] * D, 8, tag[2] * 512, 512), 8, 512
            if kind == "down":
                return wrows(w_down, tag[1] * 4 * D + tag[2] * 512, 4, 0, D), 4, D
            raise ValueError(tag)

        def seg_plan():
            P = []
            for l in (0, 1):
                if l == 1:
                    P += [("qkv", p) for p in range(3)] + [("ao", p) for p in range(2)]
                P += [("kv", l, p) for p in range(4)]
                P += [("wq", l, p) for p in range(2)] + [("wo", l, p) for p in range(2)]
                for fb in range(8):
                    P += [("up", l, fb), ("down", l, fb)]
            return P

        class WStream:
            def __init__(self, plan):
                self.plan = plan
                self.n = len(plan)
                self.next_get = 0
                self.next_load = 0
                self.released = [False] * self.n
                self.views = {}

            def top_up(self):
                while self.next_load < self.n:
                    i = self.next_load
                    if i >= NSLOT and not self.released[i - NSLOT]:
                        break
                    ap, a, b = wdesc(self.plan[i])
                    sl = i % NSLOT
                    view = ring[:, sl, 0:a * b].rearrange("p (a b) -> p a b", a=a)
                    S.dma("pool", view, ap, ringb[sl], True)
                    self.views[i] = view
                    self.next_load += 1

            def get(self, tag):
                i = self.next_get
                assert self.plan[i] == tag, (self.plan[i], tag)
                self.next_get += 1
                self.top_up()
                assert self.next_load > i, "weight ring too small at %s" % (tag,)
                return self.views[i], ringb[i % NSLOT], i

            def release(self, i):
                self.released[i] = True
                self.top_up()

        W = WStream(seg_plan() * len(segs_sel))

        with phase() as ph:
            tmpf = sb("setup_f", [128, 2048], F32, ph)
            tmpfb = S.buf("setup_f")
            S.dma("pool", ident[:], ident_d, identb, True)
            S.dma("pool", PB[:].rearrange("p a b -> p (a b)"), pb_d, PBb, True)
            S.dma("pool", masks[:].rearrange("p a b -> p (a b)"), masks_d, masksb, True)
            S.dma("sp", gfm[:], gfm_d, gfmb, True)
            S.dma("sp", gfin[:], gfinal_d.partition_broadcast(128), gfinb, True)
            S.emit("dve", lambda e: e.memset(onesb16[:], 1.0), writes=[onesb])
            S.dma("sp", sinkE[:], attn_sink, sinkb, True)
            S.emit("act", lambda e: e.activation(out=sinkE[:], in_=sinkE[:], func=AF.Exp), reads=[sinkb], writes=[sinkb])
            S.dma("sp", sinkEb[:], attn_sink.partition_broadcast(128), sinkEbb, True)
            S.emit("act", lambda e: e.activation(out=sinkEb[:], in_=sinkEb[:], func=AF.Exp), reads=[sinkEbb], writes=[sinkEbb])
            S.emit("dve", lambda e: e.memset(sinkL[0:1, 0:64], 0.0), writes=[sinkb])
            S.emit("dve", lambda e: e.memset(sinkL[0:1, 64:128], 1.0), writes=[sinkb])
            S.dma("sp", tmpf[:, 0:2048].rearrange("p (k n) -> p k n", k=8),
                  pool_w.rearrange("(k p) n -> p k n", p=128), tmpfb, True)
            sct = sb("sct", [128, D], F32, ph)
            sctb = S.buf("sct")
            S.dma("sp", sct[:], pool_scale.partition_broadcast(128), sctb, True)
            for g in range(4):
                for kc in range(2):
                    c = 2 * g + kc
                    S.emit("dve", lambda e, g=g, kc=kc, c=c: e.scalar_tensor_tensor(
                        out=Wp[:, g, kc, :], in0=tmpf[:, c * 256:(c + 1) * 256], scalar=gfm[:, c:c + 1],
                        in1=sct[:, g * 256:(g + 1) * 256], op0=ALU.mult, op1=ALU.mult),
                        reads=[tmpfb, gfmb, sctb], writes=[Wpb])
            S.barrier()

        def norm_stats(tiles, src_ap_fn, src_bufs):
            k = stat_rr[0]
            stat_rr[0] ^= 1
            n = len(tiles)
            for j in range(n):
                S.emit("act", lambda e, j=j: e.activation(out=junk[:], in_=src_ap_fn(j), func=AF.Square,
                                                          accum_out=ss[:, k, j:j + 1]),
                       reads=[src_bufs[j]], writes=[junkb, statb[k]])
            S.emit("act", lambda e: e.activation(out=lnv[:, k, 0:n], in_=ss[:, k, 0:n], func=AF.Ln,
                                                 scale=1.0 / D, bias=EPS),
                   reads=[statb[k]], writes=[statb[k]])
            S.emit("act", lambda e: e.activation(out=rstd[:, k, 0:n], in_=lnv[:, k, 0:n], func=AF.Exp, scale=-0.5),
                   reads=[statb[k]], writes=[statb[k]])
            return k

        def norm_to_hn(src_ap_fn, src_bufs):
            n = len(src_bufs)
            k = norm_stats(list(range(n)), src_ap_fn, src_bufs)
            for j in range(n):
                S.emit("dve", lambda e, j=j: e.tensor_scalar(out=hn[:, j, :], in0=src_ap_fn(j),
                                                             scalar1=rstd[:, k, j:j + 1], scalar2=None, op0=ALU.mult),
                       reads=[src_bufs[j], statb[k]], writes=[hnb[j]])
            return n

        def transpose_from_hn(n, v, dst, dstb, dst_off):
            for cp in range(4):
                bk, bb = psum()
                bv = bk[:].bitcast(BF16).rearrange("p (c t) -> p c t", c=2)
                for cc in range(2):
                    c = 2 * cp + cc
                    for j in range(n):
                        S.emit("pe", lambda e, cc=cc, c=c, j=j, bv=bv: e.transpose(
                            out=bv[:, cc, j * 128:(j + 1) * 128], in_=hn[:, j, c * 128:(c + 1) * 128], identity=ident[:]),
                            reads=[hnb[j], identb], writes=[bb], inc=(cc == 1 and j == n - 1))
                for cc in range(2):
                    c = 2 * cp + cc
                    gi = v * 8 + c
                    if cp % 2 == 0:
                        S.emit("act", lambda e, cc=cc, c=c, gi=gi, bv=bv: e.activation(
                            out=dst[:, c, dst_off:dst_off + n * 128], in_=bv[:, cc, 0:n * 128], func=AF.Identity,
                            scale=gfm[:, gi:gi + 1]), reads=[bb, gfmb], writes=[dstb])
                    else:
                        S.emit("dve", lambda e, cc=cc, c=c, gi=gi, bv=bv: e.tensor_scalar(
                            out=dst[:, c, dst_off:dst_off + n * 128], in0=bv[:, cc, 0:n * 128],
                            scalar1=gfm[:, gi:gi + 1], scalar2=None, op0=ALU.mult), reads=[bb, gfmb], writes=[dstb])

        def norm_transpose(src_ap_fn, src_bufs, v, dst, dstb, dst_off):
            n = norm_to_hn(src_ap_fn, src_bufs)
            transpose_from_hn(n, v, dst, dstb, dst_off)

        prenorm = {}

        def groups_of(tiles):
            return [tiles[i:i + 4] for i in range(0, len(tiles), 4)]

        def xsrc(tl):
            return (lambda j, tl=tl: X[:, tl[j], :]), [Xb[t] for t in tl]

        def mlp_phase(tiles, l, final_seg=None, nxt=None):
            loaded = set()
            with phase() as ph:
                ntok = len(tiles) * 128
                hT = sb("hT", [128, 8, ntok], BF16, ph)
                grps = groups_of(tiles)
                hTb = [S.buf("hT%d" % i) for i in range(len(grps))]
                u = sb("u", [128, 2, 4, 512], BF16, ph)
                ub = [S.buf("u0"), S.buf("u1")]
                def mlp_norm_a(gi):
                    f, bs = xsrc(grps[gi])
                    return norm_to_hn(f, bs)

                def mlp_norm_b(gi, nn_):
                    transpose_from_hn(nn_, 6 + l, hT, hTb[gi], gi * 512)
                if prenorm.get("mlp") is not None:
                    mlp_norm_b(0, prenorm.pop("mlp"))
                else:
                    mlp_norm_b(0, mlp_norm_a(0))

                def up(fb, gi, uu, wu, wub):
                    tl = grps[gi]
                    nt = len(tl) * 128
                    o = gi * 512
                    for s in range(4):
                        bk, bb = psum()
                        for kc in range(8):
                            S.emit("pe", lambda e, bk=bk, s=s, kc=kc: e.matmul(
                                bk[:, 0:nt], lhsT=wu[:, kc, s * 128:(s + 1) * 128], rhs=hT[:, kc, o:o + nt],
                                start=(kc == 0), stop=(kc == 7)), reads=[wub, hTb[gi]], writes=[bb], inc=(kc == 7))
                        S.emit("act", lambda e, bk=bk, s=s: e.activation(
                            out=u[:, uu, s, 0:nt], in_=bk[:, 0:nt], func=AF.Relu), reads=[bb], writes=[ub[uu]])
                        S.emit("dve", lambda e, s=s: e.tensor_tensor(
                            out=u[:, uu, s, 0:nt], in0=u[:, uu, s, 0:nt], in1=u[:, uu, s, 0:nt], op=ALU.mult),
                            reads=[ub[uu]], writes=[ub[uu]])

                def down(fb, gi, uu, wd, wdb, wdi):
                    tl = grps[gi]
                    for j, t in enumerate(tl):
                        for half in range(2):
                            bk, bb = psum()
                            for s in range(4):
                                S.emit("pe", lambda e, bk=bk, s=s, j=j, half=half: e.matmul(
                                    bk[:, :], lhsT=u[:, uu, s, j * 128:(j + 1) * 128],
                                    rhs=wd[:, s, half * 512:(half + 1) * 512], start=(s == 0), stop=(s == 3)),
                                    reads=[ub[uu], wdb], writes=[bb], inc=(s == 3))
                            S.emit("dve", lambda e, bk=bk, t=t, half=half: e.tensor_tensor(
                                out=X[:, t, half * 512:(half + 1) * 512], in0=bk[:, :],
                                in1=X[:, t, half * 512:(half + 1) * 512], op=ALU.add),
                                reads=[bb, Xb[t]], writes=[Xb[t]])
                    if fb == 7 and final_seg is not None:
                        ff_, fbs_ = xsrc(tl)
                        kf = norm_stats(tl, ff_, fbs_)
                        for j, t in enumerate(tl):
                            S.emit("dve", lambda e, j=j, t=t, kf=kf: e.scalar_tensor_tensor(
                                out=X[:, t, :], in0=X[:, t, :], scalar=rstd[:, kf, j:j + 1], in1=gfin[:, :],
                                op0=ALU.mult, op1=ALU.mult), reads=[Xb[t], statb[kf], gfinb], writes=[Xb[t]])
                            S.dma("sp", final_seg["out"](t), X[:, t, :], Xb[t], False)
                        if nxt is not None:
                            for t in tl:
                                if t < nxt["ntile"]:
                                    S.dma("sp", X[:, t, :], nxt["xin"](t), Xb[t], True)
                                    loaded.add(t)
                    if gi == len(grps) - 1:
                        W.release(wdi)

                ui = 0
                pending = None
                for fb in range(8):
                    wu, wub, wui = W.get(("up", l, fb))
                    wd, wdb, wdi = W.get(("down", l, fb))
                    for gi, tl in enumerate(grps):
                        pre = fb == 0 and gi + 1 < len(grps)
                        if pre:
                            nn_next = mlp_norm_a(gi + 1)
                        uu = ui
                        ui ^= 1
                        up(fb, gi, uu, wu, wub)
                        if pending is not None:
                            pending()
                        if pre:
                            mlp_norm_b(gi + 1, nn_next)
                        pending = (lambda fb=fb, gi=gi, uu=uu, wd=wd, wdb=wdb, wdi=wdi: down(fb, gi, uu, wd, wdb, wdi))
                    W.release(wui)
                pending()
                if l == 0:
                    wqkv3_ = [W.get(("qkv", p)) for p in range(3)]
                    for (w, wb, _i) in wqkv3_:
                        for kc in range(8):
                            S.emit("dve", lambda e, w=w, kc=kc: e.tensor_scalar(
                                out=w[:, kc, :], in0=w[:, kc, :], scalar1=gfm[:, 8 + kc:9 + kc], scalar2=None,
                                op0=ALU.mult), reads=[wb, gfmb], writes=[wb])
                    prenorm["wqkv3"] = wqkv3_
                if final_seg is not None and nxt is not None:
                    for t in range(nxt["ntile"]):
                        if t not in loaded:
                            S.dma("sp", X[:, t, :], nxt["xin"](t), Xb[t], True)
                S.barrier()

        def memkv_phase(mem_ap, l, xa_first=None):
            with phase() as ph:
                pre = prenorm.pop("mx", None) if l == 0 else None
                if pre is not None:
                    MX, MXb, pre_end = pre
                    memT = sb("memT", [128, 8, 256], BF16, ph)
                    memTb = S.buf("memT")
                    assert arena_state[0] <= pre_end - 2 * 2 * D, (arena_state[0], pre_end)
                else:
                    MX = sb("MX", [128, 2, D], F32, ph)
                    MXb = [S.buf("MX0"), S.buf("MX1")]
                    memT = sb("memT", [128, 8, 256], BF16, ph)
                    memTb = S.buf("memT")
                    for mt in range(2):
                        S.dma("sp", MX[:, mt, :], mem_ap[mt * 128:(mt + 1) * 128, :], MXb[mt], True)
                norm_transpose(lambda j: MX[:, j, :], MXb, 4 + l, memT, memTb, 0)
                if xa_first is not None:
                    f_, bs_ = xsrc(xa_first)
                    prenorm["xa"] = norm_to_hn(f_, bs_)
                for p in range(4):
                    wk, wkb, wki = W.get(("kv", l, p))
                    if p < 2:
                        for s in range(4):
                            oc = 4 * p + s
                            bk, bb = psum()
                            for kc in range(8):
                                S.emit("pe", lambda e, bk=bk, s=s, kc=kc, wk=wk: e.matmul(
                                    bk[:, 0:256], lhsT=wk[:, kc, s * 128:(s + 1) * 128], rhs=memT[:, kc, :],
                                    start=(kc == 0), stop=(kc == 7)), reads=[wkb, memTb], writes=[bb], inc=(kc == 7))
                            S.emit("act", lambda e, bk=bk, oc=oc: e.activation(
                                out=memKT[:, oc, :], in_=bk[:, 0:256], func=AF.Identity), reads=[bb], writes=[memKTb])
                    else:
                        for mt in range(2):
                            bk, bb = psum()
                            for kc in range(8):
                                S.emit("pe", lambda e, bk=bk, mt=mt, kc=kc, wk=wk: e.matmul(
                                    bk[:, :], lhsT=memT[:, kc, mt * 128:(mt + 1) * 128], rhs=wk[:, kc, :],
                                    start=(kc == 0), stop=(kc == 7)), reads=[wkb, memTb], writes=[bb], inc=(kc == 7))
                            S.emit("act", lambda e, bk=bk, mt=mt, p=p: e.activation(
                                out=memV[:, mt, (p - 2) * 512:(p - 1) * 512], in_=bk[:, :], func=AF.Identity),
                                reads=[bb], writes=[memVb])
                    W.release(wki)
                S.barrier()

        def memattn_phase(tiles, l):
            with phase() as ph:
                hTg2 = sb("hTg", [128, 2, 8, 512], BF16, ph)
                hTg2b = [S.buf("hTg0"), S.buf("hTg1")]
                qT = sb("qT", [128, 8, 512], BF16, ph)
                qTb = S.buf("qT")
                PT = sb("PT", [128, 2, 2, 512], BF16, ph)
                PTb = [S.buf("PT0"), S.buf("PT1")]
                rden = sb("rden", [128, 2, 512], F32, ph)
                rdenb = [S.buf("rden0"), S.buf("rden1")]
                oT = sb("oT", [128, 8, 512], BF16, ph)
                oTb = S.buf("oT")
                wq3 = [W.get(("wq", l, p)) for p in range(2)]
                wo3 = [W.get(("wo", l, p)) for p in range(2)]
                wq = [(a, b) for (a, b, c) in wq3]
                wo = [(a, b) for (a, b, c) in wo3]
                grps = groups_of(tiles)

                def xa_norm_a(gi):
                    f, bs = xsrc(grps[gi])
                    return norm_to_hn(f, bs)

                def xa_norm_b(gi, nn_):
                    transpose_from_hn(nn_, 2 + l, hTg2[:, gi % 2, :, :], hTg2b[gi % 2], 0)

                if prenorm.get("xa") is not None:
                    xa_norm_b(0, prenorm.pop("xa"))
                else:
                    xa_norm_b(0, xa_norm_a(0))
                for gi, tl in enumerate(grps):
                    n = len(tl)
                    nt = n * 128
                    hTg = hTg2[:, gi % 2, :, :]
                    hTgb = hTg2b[gi % 2]
                    if gi + 1 < len(grps):
                        nn_next = xa_norm_a(gi + 1)
                    for oc in range(8):
                        w, wb = wq[oc // 4]
                        bk, bb = psum()
                        for kc in range(8):
                            S.emit("pe", lambda e, bk=bk, oc=oc, kc=kc, w=w, nt=nt, hTg=hTg: e.matmul(
                                bk[:, 0:nt], lhsT=w[:, kc, (oc % 4) * 128:(oc % 4 + 1) * 128], rhs=hTg[:, kc, 0:nt],
                                start=(kc == 0), stop=(kc == 7)), reads=[wb, hTgb], writes=[bb], inc=(kc == 7))
                        S.emit("act", lambda e, bk=bk, oc=oc, nt=nt: e.activation(
                            out=qT[:, oc, 0:nt], in_=bk[:, 0:nt], func=AF.Identity), reads=[bb], writes=[qTb])
                    def stageA(h):
                        pp = h % 2
                        for mt in range(2):
                            bk, bb = psum()
                            for dc in range(2):
                                S.emit("pe", lambda e, bk=bk, h=h, mt=mt, dc=dc, nt=nt: e.matmul(
                                    bk[:, 0:nt], lhsT=memKT[:, 2 * h + dc, mt * 128:(mt + 1) * 128],
                                    rhs=qT[:, 2 * h + dc, 0:nt], start=(dc == 0), stop=(dc == 1)),
                                    reads=[memKTb, qTb], writes=[bb], inc=(dc == 1))
                            S.emit("act", lambda e, bk=bk, pp=pp, mt=mt, nt=nt: e.activation(
                                out=PT[:, pp, mt, 0:nt], in_=bk[:, 0:nt], func=AF.Exp, scale=1.0 / 16.0),
                                reads=[bb], writes=[PTb[pp]])

                    def stageB(h):
                        pp = h % 2
                        bk, bb = psum()
                        for mt in range(2):
                            S.emit("pe", lambda e, bk=bk, pp=pp, mt=mt, nt=nt: e.matmul(
                                bk[:, 0:nt], lhsT=onesb16[:, :], rhs=PT[:, pp, mt, 0:nt], start=(mt == 0), stop=(mt == 1)),
                                reads=[onesb, PTb[pp]], writes=[bb], inc=(mt == 1))
                        S.emit("act", lambda e, bk=bk, pp=pp, nt=nt: e.activation(
                            out=rden[:, pp, 0:nt], in_=bk[:, 0:nt], func=AF.Ln), reads=[bb], writes=[rdenb[pp]])
                        S.emit("act", lambda e, pp=pp, nt=nt: e.activation(
                            out=rden[:, pp, 0:nt], in_=rden[:, pp, 0:nt], func=AF.Exp, scale=-1.0),
                            reads=[rdenb[pp]], writes=[rdenb[pp]])
                        for dc in range(2):
                            bk, bb = psum()
                            for mt in range(2):
                                S.emit("pe", lambda e, bk=bk, pp=pp, mt=mt, h=h, dc=dc, nt=nt: e.matmul(
                                    bk[:, 0:nt], lhsT=memV[:, mt, h * 256 + dc * 128:h * 256 + (dc + 1) * 128],
                                    rhs=PT[:, pp, mt, 0:nt], start=(mt == 0), stop=(mt == 1)),
                                    reads=[memVb, PTb[pp]], writes=[bb], inc=(mt == 1))
                            S.emit("dve", lambda e, bk=bk, pp=pp, h=h, dc=dc, nt=nt: e.tensor_tensor(
                                out=oT[:, 2 * h + dc, 0:nt], in0=bk[:, 0:nt], in1=rden[:, pp, 0:nt], op=ALU.mult),
                                reads=[bb, rdenb[pp]], writes=[oTb])

                    stageA(0)
                    stageA(1)
                    stageB(0)
                    stageA(2)
                    stageB(1)
                    stageA(3)
                    stageB(2)
                    stageB(3)
                    if gi + 1 < len(grps):
                        xa_norm_b(gi + 1, nn_next)
                    for j, t in enumerate(tl):
                        for half in range(2):
                            w, wb = wo[half]
                            bk, bb = psum()
                            for c in range(8):
                                S.emit("pe", lambda e, bk=bk, c=c, j=j, w=w: e.matmul(
                                    bk[:, :], lhsT=oT[:, c, j * 128:(j + 1) * 128], rhs=w[:, c, :],
                                    start=(c == 0), stop=(c == 7)), reads=[oTb, wb], writes=[bb], inc=(c == 7))
                            S.emit("dve", lambda e, bk=bk, t=t, half=half: e.tensor_tensor(
                                out=X[:, t, half * 512:(half + 1) * 512], in0=bk[:, :],
                                in1=X[:, t, half * 512:(half + 1) * 512], op=ALU.add),
                                reads=[bb, Xb[t]], writes=[Xb[t]])
                for (a, b, c) in wq3 + wo3:
                    W.release(c)
                f_, bs_ = xsrc(grps[0])
                prenorm["mlp"] = norm_to_hn(f_, bs_)
                S.barrier()

        def pool_phase(seg):
            ntile = seg["ntile"]
            sample = seg["sample"]
            with phase() as ph:
                hnr = sb("hnr", [128, 4, D], BF16, ph)
                hnrb = [S.buf("hnr%d" % i) for i in range(4)]
                dT = sb("dT", [128, 2, 8, 128], BF16, ph)
                dTb = [S.buf("dT0"), S.buf("dT1")]
                if sample:
                    XE = sb("XE", [128, 2, D], F32, ph)
                    XEb = [S.buf("XE0"), S.buf("XE1")]
                    S.dma("sp", XE[:, 0, :], xs[0:128, :], XEb[0], True)
                    S.dma("sp", XE[:, 1, :], xs[19 * 128:20 * 128, :], XEb[1], True)
                    srcs = [-1] + list(range(ntile)) + [ntile]
                else:
                    srcs = list(range(ntile))

                MXp = sb("MXpre", [128, 2, D], F32, ph)
                MXpb = [S.buf("MXp0"), S.buf("MXp1")]
                for mt in range(2):
                    S.dma("sp", MXp[:, mt, :], seg["mem"][mt * 128:(mt + 1) * 128, :], MXpb[mt], True)
                prenorm["mx"] = (MXp, MXpb, arena_state[0])

                def src_of(s):
                    if s == -1:
                        return XE[:, 0, :], XEb[0]
                    if s == ntile:
                        return XE[:, 1, :], XEb[1]
                    return X[:, s, :], Xb[s]

                def blk(g, T, d):
                    if d == -1:
                        k = 0
                    elif d == 1:
                        k = 2
                    else:
                        k = 1
                        if sample:
                            if T == 1:
                                k = 5
                            elif T == ntile - 2:
                                k = 6
                        else:
                            if T == 0:
                                k = 3
                            elif T == ntile - 1:
                                k = 4
                    return PB[:, g * 7 + k, :]

                di = 0

                def do_out(T):
                    nonlocal di
                    dd = di
                    di ^= 1
                    ds = [d for d in (-1, 0, 1) if (T + d) in srcs]
                    for hb in range(2):
                        bk, bb = psum()
                        for c4 in range(4):
                            c = hb * 4 + c4
                            g = c // 2
                            for i, d in enumerate(ds):
                                sl = (T + d) % 4
                                S.emit("pe", lambda e, bk=bk, c4=c4, c=c, g=g, d=d, sl=sl, i=i, T=T: e.matmul(
                                    bk[:, c4 * 128:(c4 + 1) * 128], lhsT=hnr[:, sl, c * 128:(c + 1) * 128],
                                    rhs=blk(g, T, d), start=(i == 0), stop=(i == len(ds) - 1)),
                                    reads=[hnrb[sl], PBb], writes=[bb], inc=(c4 == 3 and i == len(ds) - 1))
                        S.emit("act", lambda e, bk=bk, hb=hb, dd=dd: e.activation(
                            out=dT[:, dd, hb * 4:(hb + 1) * 4, :].rearrange("p c t -> p (c t)"), in_=bk[:, :], func=AF.Identity),
                            reads=[bb], writes=[dTb[dd]])
                    for half in range(2):
                        bk, bb = psum()
                        for gg in range(2):
                            g = half * 2 + gg
                            for kc in range(2):
                                c = 2 * g + kc
                                S.emit("pe", lambda e, bk=bk, gg=gg, g=g, kc=kc, c=c, dd=dd: e.matmul(
                                    bk[:, gg * 256:(gg + 1) * 256], lhsT=dT[:, dd, c, :], rhs=Wp[:, g, kc, :],
                                    start=(kc == 0), stop=(kc == 1)), reads=[dTb[dd], Wpb], writes=[bb],
                                    inc=(gg == 1 and kc == 1))
                        S.emit("dve", lambda e, bk=bk, T=T, half=half: e.tensor_tensor(
                            out=X[:, T, half * 512:(half + 1) * 512], in0=bk[:, :],
                            in1=X[:, T, half * 512:(half + 1) * 512], op=ALU.add),
                            reads=[bb, Xb[T]], writes=[Xb[T]])

                pending_out = list(range(ntile))
                done_src = []
                batches = [srcs[i:i + 4] for i in range(0, len(srcs), 4)]

                def pstats(sg_):
                    aps_ = [src_of(s) for s in sg_]
                    return aps_, norm_stats(sg_, lambda j, aps_=aps_: aps_[j][0], [a[1] for a in aps_])

                nxt_stats = pstats(batches[0])
                for bi, sg in enumerate(batches):
                    aps, k = nxt_stats
                    for j, s in enumerate(sg):
                        if j == min(2, len(sg) - 1) and bi + 1 < len(batches):
                            nxt_stats = pstats(batches[bi + 1])
                        sl = s % 4
                        S.emit("dve", lambda e, j=j, sl=sl, aps=aps, k=k: e.tensor_scalar(
                            out=hnr[:, sl, :], in0=aps[j][0], scalar1=rstd[:, k, j:j + 1], scalar2=None, op0=ALU.mult),
                            reads=[aps[j][1], statb[k]], writes=[hnrb[sl]])
                        done_src.append(s)
                        while pending_out:
                            T = pending_out[0]
                            need = [T + d for d in (-1, 0, 1) if (T + d) in srcs]
                            if all(x in done_src for x in need):
                                do_out(T)
                                pending_out.pop(0)
                            else:
                                break
                assert not pending_out
                S.barrier()

        def winattn_phase(seg):
            ntile = seg["ntile"]
            sample = seg["sample"]
            qtiles = seg["core"]
            with phase() as ph:
                hTt = sb("hTt", [128, 4, 8, 128], BF16, ph)
                hTtb = [S.buf("hTt%d" % i) for i in range(4)]
                tA = sb("tA", [128, 8, 64], F32, ph)
                tB = sb("tB", [128, 8, 64], F32, ph)
                tAb = S.buf("tA")
                tBb = S.buf("tB")
                qr = sb("qr", [128, 16, 64], BF16, ph)
                qrb = S.buf("qr")
                kr = sb("kr", [128, 4, 64], BF16, ph)
                krb = S.buf("kr")
                qT = sb("qTw", [64, 16, 128], BF16, ph)
                qTb = S.buf("qTw")
                KT = sb("KT", [64, 4, 4, 128], BF16, ph)
                KTb = [S.buf("KT%d" % i) for i in range(4)]
                VA = sb("VA", [128, 4, 4, 128], BF16, ph)
                VAb = [S.buf("VA%d" % i) for i in range(4)]
                NPT = 6
                PT = sb("PTw", [128, NPT, 512], BF16, ph)
                PTb = [S.buf("PTw%d" % i) for i in range(NPT)]
                rden = sb("rdenw", [64, 2, 512], F32, ph)
                rdenb = [S.buf("rdenw0"), S.buf("rdenw1")]
                oT = sb("oTw", [128, 8, 128], BF16, ph)
                oTb = S.buf("oTw")
                RT = sb("RT", [128, 18, 64], F32, ph)
                RTb = S.buf("RT")
                nrt = 18 if sample else 16
                S.dma("sp", RT[:, 0:nrt, :].rearrange("p a b -> p (a b)"), (ropeS_d if sample else ropeP_d), RTb, True)
                for sl in range(4):
                    S.emit("dve", lambda e, sl=sl: e.memset(VA[:, sl, :, 64:128], 1.0), writes=[VAb[sl]])
                wqkv3 = prenorm.pop("wqkv3")
                wo3 = [W.get(("ao", p)) for p in range(2)]
                wqkv = [(a, b) for (a, b, c) in wqkv3]
                wo = [(a, b) for (a, b, c) in wo3]
                sinkR = sb("sinkR", [1, 16, 128], BF16, ph)
                S.emit("dve", lambda e: e.tensor_copy(out=sinkR[0:1, :, :],
                                                      in_=sinkE[0:1, :].unsqueeze(2).broadcast_to([1, 16, 128])),
                       reads=[sinkb], writes=[sinkb])
                pt_rr = [0]
                rd_rr = [0]

                def rope(src3, tab_t, dst3, nh, srcb, dstb):
                    cosb = RT[:, tab_t, 0:32].unsqueeze(1).broadcast_to([128, nh, 32])
                    sinb = RT[:, tab_t, 32:64].unsqueeze(1).broadcast_to([128, nh, 32])
                    x1 = src3[:, :, 0:32]
                    x2 = src3[:, :, 32:64]
                    A = tA[:, 0:nh, :]
                    B = tB[:, 0:nh, :]
                    S.emit("dve", lambda e: e.tensor_tensor(out=A[:, :, 0:32], in0=x1, in1=cosb, op=ALU.mult),
                           reads=[srcb, RTb], writes=[tAb])
                    S.emit("dve", lambda e: e.tensor_tensor(out=A[:, :, 32:64], in0=x2, in1=cosb, op=ALU.mult),
                           reads=[srcb, RTb], writes=[tAb])
                    S.emit("dve", lambda e: e.tensor_tensor(out=B[:, :, 0:32], in0=x2, in1=sinb, op=ALU.mult),
                           reads=[srcb, RTb], writes=[tBb])
                    S.emit("dve", lambda e: e.tensor_tensor(out=B[:, :, 32:64], in0=x1, in1=sinb, op=ALU.mult),
                           reads=[srcb, RTb], writes=[tBb])
                    S.emit("dve", lambda e: e.tensor_tensor(out=dst3[:, :, 0:32], in0=A[:, :, 0:32], in1=B[:, :, 0:32],
                                                            op=ALU.subtract), reads=[tAb, tBb], writes=[dstb])
                    S.emit("dve", lambda e: e.tensor_tensor(out=dst3[:, :, 32:64], in0=A[:, :, 32:64], in1=B[:, :, 32:64],
                                                            op=ALU.add), reads=[tAb, tBb], writes=[dstb])


                def Na(t):
                    sl = t % 4
                    k = norm_stats([t], lambda j: X[:, t, :], [Xb[t]])
                    S.emit("dve", lambda e: e.tensor_scalar(out=hn[:, sl, :], in0=X[:, t, :], scalar1=rstd[:, k, 0:1],
                                                            scalar2=None, op0=ALU.mult),
                           reads=[Xb[t], statb[k]], writes=[hnb[sl]])

                def Nb(t):
                    sl = t % 4
                    bk, bb = psum()
                    bv = bk[:].bitcast(BF16)
                    for c in range(8):
                        S.emit("pe", lambda e, c=c: e.transpose(
                            out=bv[:, c * 128:(c + 1) * 128], in_=hn[:, sl, c * 128:(c + 1) * 128], identity=ident[:]),
                            reads=[hnb[sl], identb], writes=[bb], inc=(c == 7))
                    S.emit("act", lambda e: e.activation(
                        out=hTt[:, sl, :, :].rearrange("p c t -> p (c t)"), in_=bv[:, :], func=AF.Identity),
                        reads=[bb], writes=[hTtb[sl]])

                def proj(t, nb):
                    w, wb = wqkv[nb]
                    hs = t % 4
                    bk, bb = psum()
                    for kc in range(8):
                        S.emit("pe", lambda e, bk=bk, kc=kc, w=w: e.matmul(
                            bk[:, :], lhsT=hTt[:, hs, kc, :], rhs=w[:, kc, :],
                            start=(kc == 0), stop=(kc == 7)), reads=[hTtb[hs], wb], writes=[bb], inc=(kc == 7))
                    return bk, bb

                def KVstage(t):
                    sl = t % 4
                    bk2, bb2 = proj(t, 2)
                    S.emit("act", lambda e: e.activation(
                        out=VA[:, sl, :, 0:64], in_=bk2[:, 256:512].rearrange("p (h d) -> p h d", d=64), func=AF.Identity),
                        reads=[bb2], writes=[VAb[sl]])
                    rope(bk2[:, 0:256].rearrange("p (h d) -> p h d", d=64), t, kr[:, :, :], 4, bb2, krb)

                def TKstage(t):
                    sl = t % 4
                    bk, bb = psum()
                    bv = bk[:].bitcast(BF16)
                    for kv in range(4):
                        S.emit("pe", lambda e, kv=kv: e.transpose(
                            out=bv[0:64, kv * 128:(kv + 1) * 128], in_=kr[:, kv, :], identity=ident[:]),
                            reads=[krb, identb], writes=[bb], inc=(kv == 3))
                    S.emit("act", lambda e: e.activation(
                        out=KT[:, sl, :, :].rearrange("p a b -> p (a b)"), in_=bv[0:64, 0:512], func=AF.Identity),
                        reads=[bb], writes=[KTb[sl]])

                def Qstage(t, nb):
                    bkq, bbq = proj(t, nb)
                    rope(bkq[:, :].rearrange("p (h d) -> p h d", d=64), t, qr[:, nb * 8:(nb + 1) * 8, :], 8,
                         bbq, qrb)

                def TQstage(t):
                    for nb in range(2):
                        bk, bb = psum()
                        bv = bk[:].bitcast(BF16)
                        for hh in range(8):
                            S.emit("pe", lambda e, hh=hh, nb=nb, bv=bv: e.transpose(
                                out=bv[0:64, hh * 128:(hh + 1) * 128], in_=qr[:, nb * 8 + hh, :], identity=ident[:]),
                                reads=[qrb, identb], writes=[bb], inc=(hh == 7))
                        if nb == 0:
                            S.emit("act", lambda e, bv=bv, nb=nb: e.activation(
                                out=qT[:, nb * 8:(nb + 1) * 8, :].rearrange("p a b -> p (a b)"),
                                in_=bv[0:64, 0:1024], func=AF.Identity), reads=[bb], writes=[qTb])
                        else:
                            S.emit("dve", lambda e, bv=bv, nb=nb: e.tensor_copy(
                                out=qT[:, nb * 8:(nb + 1) * 8, :].rearrange("p a b -> p (a b)"),
                                in_=bv[0:64, 0:1024]), reads=[bb], writes=[qTb])

                def attn(j, hook1, hook2, hook3):
                    keys = []
                    if j - 1 >= 0:
                        mi = 0
                        if sample and j == qtiles[0]:
                            mi = 2
                        keys.append((j - 1, mi))
                    keys.append((j, None))
                    if j + 1 < ntile:
                        mi = 1
                        if sample and j == qtiles[-1]:
                            mi = 3
                        keys.append((j + 1, mi))
                    stA = {}

                    def stageA(kv):
                        pts = []
                        for (t, mi) in keys:
                            sl = t % 4
                            bk, bb = psum()
                            S.emit("pe", lambda e, bk=bk, sl=sl, kv=kv: e.matmul(
                                bk[:, :], lhsT=KT[:, sl, kv, :],
                                rhs=qT[:, 4 * kv:4 * kv + 4, :].rearrange("p a b -> p (a b)"), start=True, stop=True),
                                reads=[KTb[sl], qTb], writes=[bb], inc=True)
                            pi = pt_rr[0]
                            pt_rr[0] = (pi + 1) % NPT
                            S.emit("act", lambda e, bk=bk, pi=pi: e.activation(
                                out=PT[:, pi, :], in_=bk[:, :], func=AF.Exp, scale=0.125), reads=[bb], writes=[PTb[pi]])
                            if mi is not None:
                                S.emit("dve", lambda e, pi=pi, mi=mi: e.tensor_tensor(
                                    out=PT[:, pi, :].rearrange("p (h q) -> p h q", h=4),
                                    in0=PT[:, pi, :].rearrange("p (h q) -> p h q", h=4),
                                    in1=masks[:, mi, :].unsqueeze(1).broadcast_to([128, 4, 128]), op=ALU.mult),
                                    reads=[PTb[pi], masksb], writes=[PTb[pi]])
                            pts.append((pi, sl))
                        stA[kv] = pts

                    def stageB(kv):
                        pts = stA[kv]
                        bk, bb = psum()
                        for i, (pi, sl) in enumerate(pts):
                            last = i == len(pts) - 1
                            S.emit("pe", lambda e, bk=bk, pi=pi, sl=sl, i=i, last=last: e.matmul(
                                bk[:, :], lhsT=VA[:, sl, kv, :], rhs=PT[:, pi, :], start=(i == 0), stop=last),
                                reads=[VAb[sl], PTb[pi]], writes=[bb], inc=last)
                        ri = rd_rr[0]
                        rd_rr[0] ^= 1
                        for hl in range(4):
                            S.emit("act", lambda e, bk=bk, ri=ri, hl=hl: e.activation(
                                out=rden[0:64, ri, hl * 128:(hl + 1) * 128], in_=bk[64:128, hl * 128:(hl + 1) * 128],
                                func=AF.Ln, bias=sinkEb[64:128, 4 * kv + hl:4 * kv + hl + 1]),
                                reads=[bb, sinkEbb], writes=[rdenb[ri]])
                        S.emit("act", lambda e, ri=ri: e.activation(
                            out=rden[0:64, ri, :], in_=rden[0:64, ri, :], func=AF.Exp, scale=-1.0),
                            reads=[rdenb[ri]], writes=[rdenb[ri]])
                        for par in range(2):
                            S.emit("dve", lambda e, bk=bk, ri=ri, par=par: e.tensor_tensor(
                                out=oT[par * 64:(par + 1) * 64, 2 * kv:2 * kv + 2, :],
                                in0=bk[0:64, :].rearrange("p (i two q) -> p i two q", two=2, q=128)[:, :, par, :],
                                in1=rden[0:64, ri, :].rearrange("p (i two q) -> p i two q", two=2, q=128)[:, :, par, :],
                                op=ALU.mult), reads=[bb, rdenb[ri]], writes=[oTb])

                    stageA(0)
                    stageA(1)
                    stageB(0)
                    stageA(2)
                    hook1()
                    stageB(1)
                    stageA(3)
                    hook2()
                    stageB(2)
                    stageB(3)
                    hook3()
                    return oproj

                def oproj(j):
                    for half in range(2):
                        w, wb = wo[half]
                        bk, bb = psum()
                        for c in range(8):
                            S.emit("pe", lambda e, bk=bk, c=c, w=w: e.matmul(
                                bk[:, :], lhsT=oT[:, c, :], rhs=w[:, c, :], start=(c == 0), stop=(c == 7)),
                                reads=[oTb, wb], writes=[bb], inc=(c == 7))
                        S.emit("dve", lambda e, bk=bk, half=half: e.tensor_tensor(
                            out=X[:, j, half * 512:(half + 1) * 512], in0=bk[:, :],
                            in1=X[:, j, half * 512:(half + 1) * 512], op=ALU.add),
                            reads=[bb, Xb[j]], writes=[Xb[j]])


                def valid(t):
                    return 0 <= t < ntile

                if valid(0):
                    Na(0)
                pending = None
                for j in range(-3, ntile):
                    if valid(j + 3):
                        Nb(j + 3)
                    if valid(j + 2):
                        KVstage(j + 2)
                    if pending is not None:
                        oproj(pending)
                        pending = None

                    def hook1(j=j):
                        if valid(j + 4):
                            Na(j + 4)
                        if valid(j + 2):
                            TKstage(j + 2)
                        if (j + 1) in qtiles:
                            Qstage(j + 1, 0)

                    def hook2(j=j):
                        if (j + 1) in qtiles:
                            Qstage(j + 1, 1)

                    def hook3(j=j):
                        if (j + 1) in qtiles:
                            TQstage(j + 1)
                    if j in qtiles:
                        attn(j, hook1, hook2, hook3)
                        pending = j
                    else:
                        hook1()
                        hook2()
                        hook3()
                if pending is not None:
                    oproj(pending)
                for (a, b, c) in wqkv3 + wo3:
                    W.release(c)
                S.barrier()

        def final_phase(seg, nxt):
            tiles = seg["core"]
            loaded = set()
            with phase() as ph:
                NY = 4
                ystage = sb("ystage", [128, NY, D], F32, ph)
                ystageb = [S.buf("ystage%d" % i) for i in range(NY)]
                evs = []
                yi = 0
                for tl in groups_of(tiles):
                    f, bs = xsrc(tl)
                    k = norm_stats(tl, f, bs)
                    for j, t in enumerate(tl):
                        S.emit("dve", lambda e, j=j, t=t, k=k, yi=yi: e.scalar_tensor_tensor(
                            out=ystage[:, yi, :], in0=X[:, t, :], scalar=rstd[:, k, j:j + 1], in1=gfin[:, :],
                            op0=ALU.mult, op1=ALU.mult), reads=[Xb[t], statb[k], gfinb], writes=[ystageb[yi]])
                        evs.append(S.dma("sp", seg["out"](t), ystage[:, yi, :], ystageb[yi], False))
                        yi = (yi + 1) % NY
                        if nxt is not None and t < nxt["ntile"]:
                            S.dma("sp", X[:, t, :], nxt["xin"](t), Xb[t], True)
                            loaded.add(t)
                if nxt is not None:
                    for t in range(nxt["ntile"]):
                        if t not in loaded:
                            S.dma("sp", X[:, t, :], nxt["xin"](t), Xb[t], True)
                for b in ystageb:
                    for e_ in ("act", "dve", "sp", "pe"):
                        S._wait(e_, list(b.r.items()))

        def dump_phase(seg):
            for t in seg["core"]:
                S.dma("sp", seg["out"](t), X[:, t, :], Xb[t], False)

        segs = []
        for b in range(2):
            segs.append(dict(sample=False, ntile=16, core=list(range(16)),
                             xin=(lambda t, b=b: xp[b, t * 128:(t + 1) * 128, :]),
                             out=(lambda t, b=b: yp[b, t * 128:(t + 1) * 128, :]),
                             mem=memp[b]))
        segs.append(dict(sample=True, ntile=18, core=list(range(1, 17)),
                         xin=(lambda t: xs[(t + 1) * 128:(t + 2) * 128, :]),
                         out=(lambda t: ys[(t - 1) * 128:t * 128, :]),
                         mem=mems))

        seglist = [segs[i] for i in segs_sel]
        for t in range(seglist[0]["ntile"]):
            S.dma("sp", X[:, t, :], seglist[0]["xin"](t), Xb[t], True)
        for si, seg in enumerate(seglist):
            nxt = seglist[si + 1] if si + 1 < len(seglist) else None
            alltiles = list(range(seg["ntile"]))
            stages = [
                ("pool", lambda: pool_phase(seg)),
                ("kv0", lambda: memkv_phase(seg["mem"], 0, alltiles[0:4])),
                ("xa0", lambda: memattn_phase(alltiles, 0)),
                ("mlp0", lambda: mlp_phase(alltiles, 0)),
                ("win", lambda: winattn_phase(seg)),
                ("kv1", lambda: memkv_phase(seg["mem"], 1, seg["core"][0:4])),
                ("xa1", lambda: memattn_phase(seg["core"], 1)),
                ("mlp1", lambda: mlp_phase(seg["core"], 1, seg if stop_after is None else None, nxt)),
            ]
            stopped = False
            for name, fn in stages:
                fn()
                if stop_after == name:
                    stopped = True
                    break
            if stopped:
                dump_phase(seg)
                if nxt is not None:
                    for t in range(nxt["ntile"]):
                        S.dma("sp", X[:, t, :], nxt["xin"](t), Xb[t], True)
            elif stop_after is not None:
                final_phase(seg, nxt)
            S.barrier()
        S.finish()
        blk = st.enter_context(nc.Block())
        S.replay(blk)
    return nc


POOL_WINDOWS = (2, 4, 8, 16)


def _pool_matrix(Sn, w):
    t = np.arange(Sn)
    lo = np.clip(t - w // 2, 0, Sn)
    hi = np.clip(t + w // 2, 0, Sn)
    cnt = (hi - lo).astype(np.float64)
    j = np.arange(Sn)[:, None]
    M = ((j >= lo[None, :]) & (j < hi[None, :])).astype(np.float64) / cnt[None, :]
    M -= np.eye(Sn)
    return M.astype(np.float32)


def _pool_blocks(first_edge, last_edge):
    out = np.zeros((128, NPB, 128), np.float32)
    for g, w in enumerate(POOL_WINDOWS):
        M = _pool_matrix(5 * 128, w)

        def B(rt, ct):
            return M[rt * 128:(rt + 1) * 128, ct * 128:(ct + 1) * 128]
        out[:, g * 7 + 0] = B(1, 2)
        out[:, g * 7 + 1] = B(2, 2)
        out[:, g * 7 + 2] = B(3, 2)
        out[:, g * 7 + 3] = B(0, 0)
        out[:, g * 7 + 4] = B(4, 4)
        out[:, g * 7 + 5] = B(0, 0) if first_edge else B(2, 2)
        out[:, g * 7 + 6] = B(4, 4) if last_edge else B(2, 2)
    return out.reshape(128, NPB * 128)


def _masks(first_edge, last_edge):
    kk = np.arange(128)[:, None]
    qq = np.arange(128)[None, :]
    prev = (kk >= qq).astype(np.float32)
    nxt = (kk <= qq).astype(np.float32)
    m = np.zeros((128, 4, 128), np.float32)
    m[:, 0] = prev
    m[:, 1] = nxt
    m[:, 2] = 0.0 if first_edge else prev
    m[:, 3] = 0.0 if last_edge else nxt
    return m.reshape(128, 4 * 128)


def _rope_table(pos0, ntile):
    inv = (np.float32(10000.0) ** (-(np.arange(0, 64, 2, dtype=np.float32) / np.float32(64.0)))).astype(np.float32)
    pos = (pos0 + np.arange(ntile * 128)).astype(np.float32)
    ang = (pos[:, None] * inv[None, :]).astype(np.float32)
    tab = np.concatenate([np.cos(ang), np.sin(ang)], axis=1).astype(np.float32)
    tab = tab.reshape(ntile, 128, 64).transpose(1, 0, 2)
    return np.ascontiguousarray(tab).reshape(128, ntile * 64)


_NC_CACHE = {}


def _get_nc(stop_after=None):
    if stop_after not in _NC_CACHE:
        _NC_CACHE[stop_after] = build_program(stop_after)
    return _NC_CACHE[stop_after]


def make_in_maps(x_prompt, x_sample, mem_prompt, mem_sample, norm_mix, pool_w, pool_scale,
                 attn_qkv, attn_o, attn_sink, norm_x, norm_mem, x_wq, x_wkv, x_wo,
                 norm_mlp, w_up, w_down, norm_final):
    f = lambda a: np.ascontiguousarray(np.asarray(a, dtype=np.float32))
    x_prompt, x_sample, mem_prompt, mem_sample = f(x_prompt), f(x_sample), f(mem_prompt), f(mem_sample)
    vecs = [f(norm_mix)[0], f(norm_mix)[1], f(norm_x)[0], f(norm_x)[1], f(norm_mem)[0], f(norm_mem)[1],
            f(norm_mlp)[0], f(norm_mlp)[1], np.zeros(D, np.float32)]
    gfm = np.stack([v.reshape(8, 128).T for v in vecs], axis=1).reshape(128, 72)
    shared = {
        "pool_w": f(pool_w).reshape(1024, 256),
        "pool_scale": f(pool_scale).reshape(1, D),
        "attn_qkv": f(attn_qkv).reshape(D, 1536),
        "attn_o": f(attn_o).reshape(D, D),
        "attn_sink": f(attn_sink).reshape(1, 16),
        "x_wq": f(x_wq).reshape(2 * D, D),
        "x_wkv": f(x_wkv).reshape(2 * D, 2 * D),
        "x_wo": f(x_wo).reshape(2 * D, D),
        "w_up": f(w_up).reshape(2 * D, 4 * D),
        "w_down": f(w_down).reshape(8 * D, D),
        "gfm": np.ascontiguousarray(gfm),
        "gfinal": f(norm_final).reshape(1, D),
        "ident": np.eye(128, dtype=np.float32),
        "ropeP": _rope_table(0, 16),
    }
    in_maps = []
    for c in range(NCORES):
        sb_, ch = c // 4, c % 4
        start = ch * 2048
        xs_ = np.zeros((2560, D), np.float32)
        lo, hi = start - 256, start + 2048 + 256
        slo, shi = max(lo, 0), min(hi, DEC_SEQ)
        xs_[slo - lo:shi - lo] = x_sample[sb_, slo:shi]
        m = dict(shared)
        m["xp"] = np.ascontiguousarray(x_prompt[2 * c:2 * c + 2])
        m["xs"] = xs_
        m["memp"] = np.ascontiguousarray(mem_prompt[2 * c:2 * c + 2])
        m["mems"] = np.ascontiguousarray(mem_sample[sb_])
        m["pbc"] = _pool_blocks(ch == 0, ch == 3)
        m["masks"] = _masks(ch == 0, ch == 3)
        m["ropeS"] = _rope_table(start - 128, 18)
        in_maps.append(m)
    return in_maps


def kernel(**inputs):
    in_maps = make_in_maps(**inputs)
    nc = _get_nc(None)
    res = run_bass_kernel_spmd(nc, in_maps, core_ids=list(range(NCORES)))
    y_prompt = np.empty((16, SEQ, D), np.float32)
    y_sample = np.empty((2, DEC_SEQ, D), np.float32)
    for c in range(NCORES):
        r = res.results[c]
        y_prompt[2 * c:2 * c + 2] = r["yp"]
        sb_, ch = c // 4, c % 4
        y_sample[sb_, ch * 2048:(ch + 1) * 2048] = r["ys"]
    return (y_prompt, y_sample)
```

```python
import numpy as np
from contextlib import ExitStack
import concourse.bass as bass
import concourse.mybir as mybir
from concourse.bass_utils import run_bass_kernel_spmd

F32 = mybir.dt.float32
BF16 = mybir.dt.bfloat16
AF = mybir.ActivationFunctionType
ALU = mybir.AluOpType

D = 1024
KC = 8
EPS = 1e-6
NCORES = 8
SEQ = 2048
DEC_SEQ = 8192
NSLOT = 6
SLOT_ELEMS = 4096
NPB = 28
DBG = {}
EMBED_WAITS = True
ARENA_EL = 24576


class Buf:
    __slots__ = ("name", "w", "r", "sem", "semcnt", "uid", "excl")
    _next = [0]

    def __init__(self, name):
        Buf._next[0] += 1
        self.uid = Buf._next[0]
        self.excl = False
        self.name = name
        self.w = None
        self.r = {}
        self.sem = None
        self.semcnt = 0


class Sched:
    ENGS = ("pe", "act", "dve", "pool", "sp")
    COMPUTE = ("pe", "act", "dve", "pool")

    def __init__(self, nc, stack):
        self.nc = nc
        self.stack = stack
        self.ops = {e: [] for e in self.ENGS}
        self.cnt = {e: 0 for e in self.ENGS}
        self.waited = {e: {} for e in self.ENGS}
        self.sems = {}
        for e in self.COMPUTE:
            self.sems[e] = stack.enter_context(nc.semaphore("sem_" + e))
        self.nbuf = 0
        self.store_events = []

    def buf(self, name):
        return Buf(name)

    def _dma_sem(self, b):
        if b.sem is None:
            self.nbuf += 1
            b.sem = self.stack.enter_context(self.nc.semaphore("dsem%d" % self.nbuf))
            self.sems[("dma", b.uid)] = b.sem
        return ("dma", b.uid)

    def _wait(self, eng, deps):
        for (key, v) in deps:
            if key == eng and eng == "pe":
                continue
            if self.waited[eng].get(key, 0) >= v:
                continue
            self.ops[eng].append(("wait", key, v))
            self.waited[eng][key] = v

    @staticmethod
    def _deps(reads, writes):
        deps = []
        for b in reads:
            if b.w is not None:
                deps.append(b.w)
            if b.excl:
                deps.extend(b.r.items())
        for b in writes:
            if b.w is not None:
                deps.append(b.w)
            deps.extend(b.r.items())
        return deps

    @staticmethod
    def _mark(ev, reads, writes):
        k, v = ev
        for b in reads:
            if b.r.get(k, 0) < v:
                b.r[k] = v
        for b in writes:
            b.w = ev
            b.r = {}

    def emit(self, eng, fn, reads=(), writes=(), inc=True):
        self._wait(eng, self._deps(reads, writes))
        if inc:
            self.cnt[eng] += 1
            ev = (eng, self.cnt[eng])
        else:
            assert eng == "pe"
            ev = (eng, self.cnt[eng] + 1)
        self.ops[eng].append(("op", fn, inc))
        self._mark(ev, reads, writes)
        return ev

    def dma(self, eng, out_ap, in_ap, b, load):
        key = self._dma_sem(b)
        reads = [] if load else [b]
        writes = [b] if load else []
        self._wait(eng, self._deps(reads, writes))
        b.semcnt += 16
        ev = (key, b.semcnt)
        self.ops[eng].append(("dma", out_ap, in_ap, b.sem))
        self._mark(ev, reads, writes)
        if not load:
            self.store_events.append(ev)
        return ev

    def barrier(self):
        evs = [(e, self.cnt[e]) for e in ("pe", "act", "dve") if self.cnt[e] > 0]
        for e in ("pe", "act", "dve", "sp"):
            self._wait(e, [x for x in evs if x[0] != e])

    def finish(self):
        self._wait("sp", self.store_events)

    def replay(self, block):
        sems = self.sems

        def run(engname, e):
            pending = []
            for op in self.ops[engname]:
                if op[0] == "wait":
                    pending.append(op)
                elif op[0] == "op":
                    emb = pending.pop() if (pending and EMBED_WAITS) else None
                    for w in pending:
                        e.wait_ge(sems[w[1]], w[2])
                    pending = []
                    ins = op[1](e)
                    if emb is not None:
                        ins._wait_ge(sems[emb[1]], emb[2])
                    if op[2]:
                        ins.then_inc(sems[engname], 1)
                else:
                    for w in pending:
                        e.wait_ge(sems[w[1]], w[2])
                    pending = []
                    e.dma_start(out=op[1], in_=op[2]).then_inc(op[3], 16)
            for w in pending:
                e.wait_ge(sems[w[1]], w[2])

        @block.sync
        def _(e):
            run("sp", e)

        @block.tensor
        def _(e):
            run("pe", e)

        @block.scalar
        def _(e):
            run("act", e)

        @block.vector
        def _(e):
            run("dve", e)

        @block.gpsimd
        def _(e):
            run("pool", e)


def build_program(stop_after=None, segs_sel=(0, 1, 2)):
    nc = bass.Bass("TRN2", target_bir_lowering=False)

    def din(name, shape):
        return nc.dram_tensor(name, list(shape), F32, kind="ExternalInput").ap()

    xp = din("xp", [2, SEQ, D])
    xs = din("xs", [2560, D])
    memp = din("memp", [2, 256, D])
    mems = din("mems", [256, D])
    pool_w = din("pool_w", [1024, 256])
    pool_scale = din("pool_scale", [1, D])
    attn_qkv = din("attn_qkv", [D, 1536])
    attn_o = din("attn_o", [D, D])
    attn_sink = din("attn_sink", [1, 16])
    x_wq = din("x_wq", [2 * D, D])
    x_wkv = din("x_wkv", [2 * D, 2 * D])
    x_wo = din("x_wo", [2 * D, D])
    w_up = din("w_up", [2 * D, 4 * D])
    w_down = din("w_down", [8 * D, D])
    gfm_d = din("gfm", [128, 72])
    gfinal_d = din("gfinal", [1, D])
    ident_d = din("ident", [128, 128])
    pb_d = din("pbc", [128, NPB * 128])
    masks_d = din("masks", [128, 4 * 128])
    ropeP_d = din("ropeP", [128, 16 * 64])
    ropeS_d = din("ropeS", [128, 18 * 64])
    yp = nc.dram_tensor("yp", [2, SEQ, D], F32, kind="ExternalOutput").ap()
    ys = nc.dram_tensor("ys", [SEQ, D], F32, kind="ExternalOutput").ap()

    with ExitStack() as st:
        S = Sched(nc, st)

        arena_state = [0]

        def sb(name, shape, dt, stack=st):
            if stack is st:
                return st.enter_context(nc.sbuf_tensor("s_" + name, list(shape), dt))
            n = 1
            for d in shape[1:]:
                n *= d
            el = 2 * n if dt == F32 else n
            off = (arena_state[0] + 31) // 32 * 32
            assert off + el <= ARENA_EL, (name, off, el)
            arena_state[0] = off + el
            ap = arena[0:shape[0], off:off + el]
            if dt == F32:
                ap = ap.bitcast(F32)
            if len(shape) > 2:
                names = ["d%d" % i for i in range(len(shape) - 1)]
                kw = {nm: d for nm, d in zip(names, shape[1:])}
                ap = ap.rearrange("p (%s) -> p %s" % (" ".join(names), " ".join(names)), **kw)
            return ap

        class phase:
            def __enter__(self):
                arena_state[0] = 0
                return "arena"

            def __exit__(self, *a):
                return False

        arena = st.enter_context(nc.sbuf_tensor("arena", [128, ARENA_EL], BF16))
        X = sb("X", [128, 18, D], F32)
        Xb = [S.buf("X%d" % i) for i in range(18)]
        ring = sb("ring", [128, NSLOT, SLOT_ELEMS], BF16)
        ringb = [S.buf("ring%d" % i) for i in range(NSLOT)]
        memKT = sb("memKT", [128, 8, 256], BF16)
        memKTb = S.buf("memKT")
        memV = sb("memV", [128, 2, D], BF16)
        memVb = S.buf("memV")
        Wp = sb("Wp", [128, 4, 2, 256], BF16)
        Wpb = S.buf("Wp")
        PB = sb("PB", [128, NPB, 128], BF16)
        PBb = S.buf("PB")
        gfin = sb("gfin", [128, D], F32)
        gfinb = S.buf("gfin")
        ident = sb("ident", [128, 128], BF16)
        identb = S.buf("ident")
        masks = sb("masks", [128, 4, 128], BF16)
        masksb = S.buf("masks")
        gfm = sb("gfm", [128, 72], F32)
        gfmb = S.buf("gfm")
        junk = sb("junk", [128, D], BF16)
        junkb = S.buf("junk")
        hn = sb("hn", [128, 4, D], BF16)
        hnb = [S.buf("hn%d" % i) for i in range(4)]
        ss = sb("ss", [128, 2, 4], F32)
        lnv = sb("lnv", [128, 2, 4], F32)
        rstd = sb("rstd", [128, 2, 4], F32)
        statb = [S.buf("stat0"), S.buf("stat1")]
        onesb16 = sb("ones", [128, 128], BF16)
        onesb = S.buf("ones")
        sinkL = sb("sinkL", [1, 128], BF16)
        sinkE = sb("sinkE", [1, 16], F32)
        sinkEb = sb("sinkEb", [128, 16], F32)
        sinkEbb = S.buf("sinkEb")
        sinkb = S.buf("sink")

        banks = [st.enter_context(nc.psum_tensor("bank%d" % i, [128, 512], F32)) for i in range(8)]
        bankb = [S.buf("bank%d" % i) for i in range(8)]
        for b_ in bankb:
            b_.excl = True
        bank_rr = [0]

        def psum():
            i = bank_rr[0]
            bank_rr[0] = (i + 1) % 8
            return banks[i], bankb[i]

        stat_rr = [0]

        def wrows(w, r0, nrow_chunks, c0, ncols):
            return w[r0:r0 + nrow_chunks * 128, c0:c0 + ncols].rearrange("(k p) n -> p k n", p=128)

        def wdesc(tag):
            kind = tag[0]
            if kind == "qkv":
                return wrows(attn_qkv, 0, 8, tag[1] * 512, 512), 8, 512
            if kind == "ao":
                return wrows(attn_o, 0, 8, tag[1] * 512, 512), 8, 512
            if kind == "kv":
                return wrows(x_wkv, tag[1] * D, 8, tag[2] * 512, 512), 8, 512
            if kind == "wq":
                return wrows(x_wq, tag[1] * D, 8, tag[2] * 512, 512), 8, 512
            if kind == "wo":
                return wrows(x_wo, tag[1] * D, 8, tag[2] * 512, 512), 8, 512
            if kind == "up":
                return wrows(w_up, tag[1] * D, 8, tag[2] * 512, 512), 8, 512
            if kind == "down":
                return wrows(w_down, tag[1] * 4 * D + tag[2] * 512, 4, 0, D), 4, D
            raise ValueError(tag)

        def seg_plan():
            P = []
            for l in (0, 1):
                if l == 1:
                    P += [("qkv", p) for p in range(3)] + [("ao", p) for p in range(2)]
                P += [("kv", l, p) for p in range(4)]
                P += [("wq", l, p) for p in range(2)] + [("wo", l, p) for p in range(2)]
                for fb in range(8):
                    P += [("up", l, fb), ("down", l, fb)]
            return P

        class WStream:
            def __init__(self, plan):
                self.plan = plan
                self.n = len(plan)
                self.next_get = 0
                self.next_load = 0
                self.released = [False] * self.n
                self.views = {}

            def top_up(self):
                while self.next_load < self.n:
                    i = self.next_load
                    if i >= NSLOT and not self.released[i - NSLOT]:
                        break
                    ap, a, b = wdesc(self.plan[i])
                    sl = i % NSLOT
                    view = ring[:, sl, 0:a * b].rearrange("p (a b) -> p a b", a=a)
                    S.dma("pool", view, ap, ringb[sl], True)
                    self.views[i] = view
                    self.next_load += 1

            def get(self, tag):
                i = self.next_get
                assert self.plan[i] == tag, (self.plan[i], tag)
                self.next_get += 1
                self.top_up()
                assert self.next_load > i, "weight ring too small at %s" % (tag,)
                return self.views[i], ringb[i % NSLOT], i

            def release(self, i):
                self.released[i] = True
                self.top_up()

        W = WStream(seg_plan() * len(segs_sel))

        with phase() as ph:
            tmpf = sb("setup_f", [128, 2048], F32, ph)
            tmpfb = S.buf("setup_f")
            S.dma("pool", ident[:], ident_d, identb, True)
            S.dma("pool", PB[:].rearrange("p a b -> p (a b)"), pb_d, PBb, True)
            S.dma("pool", masks[:].rearrange("p a b -> p (a b)"), masks_d, masksb, True)
            S.dma("sp", gfm[:], gfm_d, gfmb, True)
            S.dma("sp", gfin[:], gfinal_d.partition_broadcast(128), gfinb, True)
            S.emit("dve", lambda e: e.memset(onesb16[:], 1.0), writes=[onesb])
            S.dma("sp", sinkE[:], attn_sink, sinkb, True)
            S.emit("act", lambda e: e.activation(out=sinkE[:], in_=sinkE[:], func=AF.Exp), reads=[sinkb], writes=[sinkb])
            S.dma("sp", sinkEb[:], attn_sink.partition_broadcast(128), sinkEbb, True)
            S.emit("act", lambda e: e.activation(out=sinkEb[:], in_=sinkEb[:], func=AF.Exp), reads=[sinkEbb], writes=[sinkEbb])
            S.emit("dve", lambda e: e.memset(sinkL[0:1, 0:64], 0.0), writes=[sinkb])
            S.emit("dve", lambda e: e.memset(sinkL[0:1, 64:128], 1.0), writes=[sinkb])
            S.dma("sp", tmpf[:, 0:2048].rearrange("p (k n) -> p k n", k=8),
                  pool_w.rearrange("(k p) n -> p k n", p=128), tmpfb, True)
            sct = sb("sct", [128, D], F32, ph)
            sctb = S.buf("sct")
            S.dma("sp", sct[:], pool_scale.partition_broadcast(128), sctb, True)
            for g in range(4):
                for kc in range(2):
                    c = 2 * g + kc
                    S.emit("dve", lambda e, g=g, kc=kc, c=c: e.scalar_tensor_tensor(
                        out=Wp[:, g, kc, :], in0=tmpf[:, c * 256:(c + 1) * 256], scalar=gfm[:, c:c + 1],
                        in1=sct[:, g * 256:(g + 1) * 256], op0=ALU.mult, op1=ALU.mult),
                        reads=[tmpfb, gfmb, sctb], writes=[Wpb])
            S.barrier()

        def norm_stats(tiles, src_ap_fn, src_bufs):
            k = stat_rr[0]
            stat_rr[0] ^= 1
            n = len(tiles)
            for j in range(n):
                S.emit("act", lambda e, j=j: e.activation(out=junk[:], in_=src_ap_fn(j), func=AF.Square,
                                                          accum_out=ss[:, k, j:j + 1]),
                       reads=[src_bufs[j]], writes=[junkb, statb[k]])
            S.emit("act", lambda e: e.activation(out=lnv[:, k, 0:n], in_=ss[:, k, 0:n], func=AF.Ln,
                                                 scale=1.0 / D, bias=EPS),
                   reads=[statb[k]], writes=[statb[k]])
            S.emit("act", lambda e: e.activation(out=rstd[:, k, 0:n], in_=lnv[:, k, 0:n], func=AF.Exp, scale=-0.5),
                   reads=[statb[k]], writes=[statb[k]])
            return k

        def norm_to_hn(src_ap_fn, src_bufs):
            n = len(src_bufs)
            k = norm_stats(list(range(n)), src_ap_fn, src_bufs)
            for j in range(n):
                S.emit("dve", lambda e, j=j: e.tensor_scalar(out=hn[:, j, :], in0=src_ap_fn(j),
                                                             scalar1=rstd[:, k, j:j + 1], scalar2=None, op0=ALU.mult),
                       reads=[src_bufs[j], statb[k]], writes=[hnb[j]])
            return n

        def transpose_from_hn(n, v, dst, dstb, dst_off):
            for cp in range(4):
                bk, bb = psum()
                bv = bk[:].bitcast(BF16).rearrange("p (c t) -> p c t", c=2)
                for cc in range(2):
                    c = 2 * cp + cc
                    for j in range(n):
                        S.emit("pe", lambda e, cc=cc, c=c, j=j, bv=bv: e.transpose(
                            out=bv[:, cc, j * 128:(j + 1) * 128], in_=hn[:, j, c * 128:(c + 1) * 128], identity=ident[:]),
                            reads=[hnb[j], identb], writes=[bb], inc=(cc == 1 and j == n - 1))
                for cc in range(2):
                    c = 2 * cp + cc
                    gi = v * 8 + c
                    if cp % 2 == 0:
                        S.emit("act", lambda e, cc=cc, c=c, gi=gi, bv=bv: e.activation(
                            out=dst[:, c, dst_off:dst_off + n * 128], in_=bv[:, cc, 0:n * 128], func=AF.Identity,
                            scale=gfm[:, gi:gi + 1]), reads=[bb, gfmb], writes=[dstb])
                    else:
                        S.emit("dve", lambda e, cc=cc, c=c, gi=gi, bv=bv: e.tensor_scalar(
                            out=dst[:, c, dst_off:dst_off + n * 128], in0=bv[:, cc, 0:n * 128],
                            scalar1=gfm[:, gi:gi + 1], scalar2=None, op0=ALU.mult), reads=[bb, gfmb], writes=[dstb])

        def norm_transpose(src_ap_fn, src_bufs, v, dst, dstb, dst_off):
            n = norm_to_hn(src_ap_fn, src_bufs)
            transpose_from_hn(n, v, dst, dstb, dst_off)

        prenorm = {}

        def groups_of(tiles):
            return [tiles[i:i + 4] for i in range(0, len(tiles), 4)]

        def xsrc(tl):
            return (lambda j, tl=tl: X[:, tl[j], :]), [Xb[t] for t in tl]

        def mlp_phase(tiles, l, final_seg=None, nxt=None):
            loaded = set()
            with phase() as ph:
                ntok = len(tiles) * 128
                hT = sb("hT", [128, 8, ntok], BF16, ph)
                grps = groups_of(tiles)
                hTb = [S.buf("hT%d" % i) for i in range(len(grps))]
                u = sb("u", [128, 2, 4, 512], BF16, ph)
                ub = [S.buf("u0"), S.buf("u1")]
                def mlp_norm_a(gi):
                    f, bs = xsrc(grps[gi])
                    return norm_to_hn(f, bs)

                def mlp_norm_b(gi, nn_):
                    transpose_from_hn(nn_, 6 + l, hT, hTb[gi], gi * 512)
                if prenorm.get("mlp") is not None:
                    mlp_norm_b(0, prenorm.pop("mlp"))
                else:
                    mlp_norm_b(0, mlp_norm_a(0))

                def up(fb, gi, uu, wu, wub):
                    tl = grps[gi]
                    nt = len(tl) * 128
                    o = gi * 512
                    for s in range(4):
                        bk, bb = psum()
                        for kc in range(8):
                            S.emit("pe", lambda e, bk=bk, s=s, kc=kc: e.matmul(
                                bk[:, 0:nt], lhsT=wu[:, kc, s * 128:(s + 1) * 128], rhs=hT[:, kc, o:o + nt],
                                start=(kc == 0), stop=(kc == 7)), reads=[wub, hTb[gi]], writes=[bb], inc=(kc == 7))
                        S.emit("act", lambda e, bk=bk, s=s: e.activation(
                            out=u[:, uu, s, 0:nt], in_=bk[:, 0:nt], func=AF.Relu), reads=[bb], writes=[ub[uu]])
                        S.emit("dve", lambda e, s=s: e.tensor_tensor(
                            out=u[:, uu, s, 0:nt], in0=u[:, uu, s, 0:nt], in1=u[:, uu, s, 0:nt], op=ALU.mult),
                            reads=[ub[uu]], writes=[ub[uu]])

                def down(fb, gi, uu, wd, wdb, wdi):
                    tl = grps[gi]
                    for j, t in enumerate(tl):
                        for half in range(2):
                            bk, bb = psum()
                            for s in range(4):
                                S.emit("pe", lambda e, bk=bk, s=s, j=j, half=half: e.matmul(
                                    bk[:, :], lhsT=u[:, uu, s, j * 128:(j + 1) * 128],
                                    rhs=wd[:, s, half * 512:(half + 1) * 512], start=(s == 0), stop=(s == 3)),
                                    reads=[ub[uu], wdb], writes=[bb], inc=(s == 3))
                            S.emit("dve", lambda e, bk=bk, t=t, half=half: e.tensor_tensor(
                                out=X[:, t, half * 512:(half + 1) * 512], in0=bk[:, :],
                                in1=X[:, t, half * 512:(half + 1) * 512], op=ALU.add),
                                reads=[bb, Xb[t]], writes=[Xb[t]])
                    if fb == 7 and final_seg is not None:
                        ff_, fbs_ = xsrc(tl)
                        kf = norm_stats(tl, ff_, fbs_)
                        for j, t in enumerate(tl):
                            S.emit("dve", lambda e, j=j, t=t, kf=kf: e.scalar_tensor_tensor(
                                out=X[:, t, :], in0=X[:, t, :], scalar=rstd[:, kf, j:j + 1], in1=gfin[:, :],
                                op0=ALU.mult, op1=ALU.mult), reads=[Xb[t], statb[kf], gfinb], writes=[Xb[t]])
                            S.dma("sp", final_seg["out"](t), X[:, t, :], Xb[t], False)
                        if nxt is not None:
                            for t in tl:
                                if t < nxt["ntile"]:
                                    S.dma("sp", X[:, t, :], nxt["xin"](t), Xb[t], True)
                                    loaded.add(t)
                    if gi == len(grps) - 1:
                        W.release(wdi)

                ui = 0
                pending = None
                for fb in range(8):
                    wu, wub, wui = W.get(("up", l, fb))
                    wd, wdb, wdi = W.get(("down", l, fb))
                    for gi, tl in enumerate(grps):
                        pre = fb == 0 and gi + 1 < len(grps)
                        if pre:
                            nn_next = mlp_norm_a(gi + 1)
                        uu = ui
                        ui ^= 1
                        up(fb, gi, uu, wu, wub)
                        if pending is not None:
                            pending()
                        if pre:
                            mlp_norm_b(gi + 1, nn_next)
                        pending = (lambda fb=fb, gi=gi, uu=uu, wd=wd, wdb=wdb, wdi=wdi: down(fb, gi, uu, wd, wdb, wdi))
                    W.release(wui)
                pending()
                if l == 0:
                    wqkv3_ = [W.get(("qkv", p)) for p in range(3)]
                    for (w, wb, _i) in wqkv3_:
                        for kc in range(8):
                            S.emit("dve", lambda e, w=w, kc=kc: e.tensor_scalar(
                                out=w[:, kc, :], in0=w[:, kc, :], scalar1=gfm[:, 8 + kc:9 + kc], scalar2=None,
                                op0=ALU.mult), reads=[wb, gfmb], writes=[wb])
                    prenorm["wqkv3"] = wqkv3_
                    for t_ in tiles[0:2]:
                        k_ = norm_stats([t_], lambda j, t_=t_: X[:, t_, :], [Xb[t_]])
                        S.emit("dve", lambda e, t_=t_, k_=k_: e.tensor_scalar(
                            out=hn[:, t_ % 4, :], in0=X[:, t_, :], scalar1=rstd[:, k_, 0:1], scalar2=None,
                            op0=ALU.mult), reads=[Xb[t_], statb[k_]], writes=[hnb[t_ % 4]])
                    prenorm["win_na"] = set(tiles[0:2])
                if final_seg is not None and nxt is not None:
                    for t in range(nxt["ntile"]):
                        if t not in loaded:
                            S.dma("sp", X[:, t, :], nxt["xin"](t), Xb[t], True)
                S.barrier()

        def memkv_phase(mem_ap, l, xa_first=None):
            with phase() as ph:
                pre = prenorm.pop("mx", None) if l == 0 else None
                if pre is not None:
                    MX, MXb, pre_end = pre
                    memT = sb("memT", [128, 8, 256], BF16, ph)
                    memTb = S.buf("memT")
                    assert arena_state[0] <= pre_end - 2 * 2 * D, (arena_state[0], pre_end)
                else:
                    MX = sb("MX", [128, 2, D], F32, ph)
                    MXb = [S.buf("MX0"), S.buf("MX1")]
                    memT = sb("memT", [128, 8, 256], BF16, ph)
                    memTb = S.buf("memT")
                    for mt in range(2):
                        S.dma("sp", MX[:, mt, :], mem_ap[mt * 128:(mt + 1) * 128, :], MXb[mt], True)
                if pre is not None and prenorm.get("kvn") is not None:
                    transpose_from_hn(prenorm.pop("kvn"), 4 + l, memT, memTb, 0)
                else:
                    norm_transpose(lambda j: MX[:, j, :], MXb, 4 + l, memT, memTb, 0)
                if xa_first is not None:
                    f_, bs_ = xsrc(xa_first)
                    prenorm["xa"] = norm_to_hn(f_, bs_)
                for p in range(4):
                    wk, wkb, wki = W.get(("kv", l, p))
                    if p < 2:
                        for s in range(4):
                            oc = 4 * p + s
                            bk, bb = psum()
                            for kc in range(8):
                                S.emit("pe", lambda e, bk=bk, s=s, kc=kc, wk=wk: e.matmul(
                                    bk[:, 0:256], lhsT=wk[:, kc, s * 128:(s + 1) * 128], rhs=memT[:, kc, :],
                                    start=(kc == 0), stop=(kc == 7)), reads=[wkb, memTb], writes=[bb], inc=(kc == 7))
                            S.emit("act", lambda e, bk=bk, oc=oc: e.activation(
                                out=memKT[:, oc, :], in_=bk[:, 0:256], func=AF.Identity), reads=[bb], writes=[memKTb])
                    else:
                        for mt in range(2):
                            bk, bb = psum()
                            for kc in range(8):
                                S.emit("pe", lambda e, bk=bk, mt=mt, kc=kc, wk=wk: e.matmul(
                                    bk[:, :], lhsT=memT[:, kc, mt * 128:(mt + 1) * 128], rhs=wk[:, kc, :],
                                    start=(kc == 0), stop=(kc == 7)), reads=[wkb, memTb], writes=[bb], inc=(kc == 7))
                            S.emit("act", lambda e, bk=bk, mt=mt, p=p: e.activation(
                                out=memV[:, mt, (p - 2) * 512:(p - 1) * 512], in_=bk[:, :], func=AF.Identity),
                                reads=[bb], writes=[memVb])
                    W.release(wki)
                S.barrier()

        def memattn_phase(tiles, l):
            with phase() as ph:
                hTg2 = sb("hTg", [128, 2, 8, 512], BF16, ph)
                hTg2b = [S.buf("hTg0"), S.buf("hTg1")]
                qT = sb("qT", [128, 8, 512], BF16, ph)
                qTb = S.buf("qT")
                PT = sb("PT", [128, 2, 2, 512], BF16, ph)
                PTb = [S.buf("PT0"), S.buf("PT1")]
                rden = sb("rden", [128, 2, 512], F32, ph)
                rdenb = [S.buf("rden0"), S.buf("rden1")]
                oT = sb("oT", [128, 8, 512], BF16, ph)
                oTb = S.buf("oT")
                wq3 = [W.get(("wq", l, p)) for p in range(2)]
                wo3 = [W.get(("wo", l, p)) for p in range(2)]
                wq = [(a, b) for (a, b, c) in wq3]
                wo = [(a, b) for (a, b, c) in wo3]
                grps = groups_of(tiles)

                def xa_norm_a(gi):
                    f, bs = xsrc(grps[gi])
                    return norm_to_hn(f, bs)

                def xa_norm_b(gi, nn_):
                    transpose_from_hn(nn_, 2 + l, hTg2[:, gi % 2, :, :], hTg2b[gi % 2], 0)

                if prenorm.get("xa") is not None:
                    xa_norm_b(0, prenorm.pop("xa"))
                else:
                    xa_norm_b(0, xa_norm_a(0))
                for gi, tl in enumerate(grps):
                    n = len(tl)
                    nt = n * 128
                    hTg = hTg2[:, gi % 2, :, :]
                    hTgb = hTg2b[gi % 2]
                    if gi + 1 < len(grps):
                        nn_next = xa_norm_a(gi + 1)
                    for oc in range(8):
                        w, wb = wq[oc // 4]
                        bk, bb = psum()
                        for kc in range(8):
                            S.emit("pe", lambda e, bk=bk, oc=oc, kc=kc, w=w, nt=nt, hTg=hTg: e.matmul(
                                bk[:, 0:nt], lhsT=w[:, kc, (oc % 4) * 128:(oc % 4 + 1) * 128], rhs=hTg[:, kc, 0:nt],
                                start=(kc == 0), stop=(kc == 7)), reads=[wb, hTgb], writes=[bb], inc=(kc == 7))
                        S.emit("act", lambda e, bk=bk, oc=oc, nt=nt: e.activation(
                            out=qT[:, oc, 0:nt], in_=bk[:, 0:nt], func=AF.Identity), reads=[bb], writes=[qTb])
                    def stageA(h):
                        pp = h % 2
                        for mt in range(2):
                            bk, bb = psum()
                            for dc in range(2):
                                S.emit("pe", lambda e, bk=bk, h=h, mt=mt, dc=dc, nt=nt: e.matmul(
                                    bk[:, 0:nt], lhsT=memKT[:, 2 * h + dc, mt * 128:(mt + 1) * 128],
                                    rhs=qT[:, 2 * h + dc, 0:nt], start=(dc == 0), stop=(dc == 1)),
                                    reads=[memKTb, qTb], writes=[bb], inc=(dc == 1))
                            S.emit("act", lambda e, bk=bk, pp=pp, mt=mt, nt=nt: e.activation(
                                out=PT[:, pp, mt, 0:nt], in_=bk[:, 0:nt], func=AF.Exp, scale=1.0 / 16.0),
                                reads=[bb], writes=[PTb[pp]])

                    def stageB(h):
                        pp = h % 2
                        bk, bb = psum()
                        for mt in range(2):
                            S.emit("pe", lambda e, bk=bk, pp=pp, mt=mt, nt=nt: e.matmul(
                                bk[:, 0:nt], lhsT=onesb16[:, :], rhs=PT[:, pp, mt, 0:nt], start=(mt == 0), stop=(mt == 1)),
                                reads=[onesb, PTb[pp]], writes=[bb], inc=(mt == 1))
                        S.emit("act", lambda e, bk=bk, pp=pp, nt=nt: e.activation(
                            out=rden[:, pp, 0:nt], in_=bk[:, 0:nt], func=AF.Ln), reads=[bb], writes=[rdenb[pp]])
                        S.emit("act", lambda e, pp=pp, nt=nt: e.activation(
                            out=rden[:, pp, 0:nt], in_=rden[:, pp, 0:nt], func=AF.Exp, scale=-1.0),
                            reads=[rdenb[pp]], writes=[rdenb[pp]])
                        for dc in range(2):
                            bk, bb = psum()
                            for mt in range(2):
                                S.emit("pe", lambda e, bk=bk, pp=pp, mt=mt, h=h, dc=dc, nt=nt: e.matmul(
                                    bk[:, 0:nt], lhsT=memV[:, mt, h * 256 + dc * 128:h * 256 + (dc + 1) * 128],
                                    rhs=PT[:, pp, mt, 0:nt], start=(mt == 0), stop=(mt == 1)),
                                    reads=[memVb, PTb[pp]], writes=[bb], inc=(mt == 1))
                            S.emit("dve", lambda e, bk=bk, pp=pp, h=h, dc=dc, nt=nt: e.tensor_tensor(
                                out=oT[:, 2 * h + dc, 0:nt], in0=bk[:, 0:nt], in1=rden[:, pp, 0:nt], op=ALU.mult),
                                reads=[bb, rdenb[pp]], writes=[oTb])

                    stageA(0)
                    stageA(1)
                    stageB(0)
                    stageA(2)
                    stageB(1)
                    stageA(3)
                    stageB(2)
                    stageB(3)
                    if gi + 1 < len(grps):
                        xa_norm_b(gi + 1, nn_next)
                    for j, t in enumerate(tl):
                        for half in range(2):
                            w, wb = wo[half]
                            bk, bb = psum()
                            for c in range(8):
                                S.emit("pe", lambda e, bk=bk, c=c, j=j, w=w: e.matmul(
                                    bk[:, :], lhsT=oT[:, c, j * 128:(j + 1) * 128], rhs=w[:, c, :],
                                    start=(c == 0), stop=(c == 7)), reads=[oTb, wb], writes=[bb], inc=(c == 7))
                            S.emit("dve", lambda e, bk=bk, t=t, half=half: e.tensor_tensor(
                                out=X[:, t, half * 512:(half + 1) * 512], in0=bk[:, :],
                                in1=X[:, t, half * 512:(half + 1) * 512], op=ALU.add),
                                reads=[bb, Xb[t]], writes=[Xb[t]])
                for (a, b, c) in wq3 + wo3:
                    W.release(c)
                f_, bs_ = xsrc(grps[0])
                prenorm["mlp"] = norm_to_hn(f_, bs_)
                S.barrier()

        def pool_phase(seg):
            ntile = seg["ntile"]
            sample = seg["sample"]
            with phase() as ph:
                hnr = sb("hnr", [128, 4, D], BF16, ph)
                hnrb = [S.buf("hnr%d" % i) for i in range(4)]
                dT = sb("dT", [128, 2, 8, 128], BF16, ph)
                dTb = [S.buf("dT0"), S.buf("dT1")]
                if sample:
                    XE = sb("XE", [128, 2, D], F32, ph)
                    XEb = [S.buf("XE0"), S.buf("XE1")]
                    S.dma("sp", XE[:, 0, :], xs[0:128, :], XEb[0], True)
                    S.dma("sp", XE[:, 1, :], xs[19 * 128:20 * 128, :], XEb[1], True)
                    srcs = [-1] + list(range(ntile)) + [ntile]
                else:
                    srcs = list(range(ntile))

                MXp = sb("MXpre", [128, 2, D], F32, ph)
                MXpb = [S.buf("MXp0"), S.buf("MXp1")]
                for mt in range(2):
                    S.dma("sp", MXp[:, mt, :], seg["mem"][mt * 128:(mt + 1) * 128, :], MXpb[mt], True)
                prenorm["mx"] = (MXp, MXpb, arena_state[0])

                def src_of(s):
                    if s == -1:
                        return XE[:, 0, :], XEb[0]
                    if s == ntile:
                        return XE[:, 1, :], XEb[1]
                    return X[:, s, :], Xb[s]

                def blk(g, T, d):
                    if d == -1:
                        k = 0
                    elif d == 1:
                        k = 2
                    else:
                        k = 1
                        if sample:
                            if T == 1:
                                k = 5
                            elif T == ntile - 2:
                                k = 6
                        else:
                            if T == 0:
                                k = 3
                            elif T == ntile - 1:
                                k = 4
                    return PB[:, g * 7 + k, :]

                di = 0

                def do_out(T):
                    nonlocal di
                    dd = di
                    di ^= 1
                    ds = [d for d in (-1, 0, 1) if (T + d) in srcs]
                    for hb in range(2):
                        bk, bb = psum()
                        for c4 in range(4):
                            c = hb * 4 + c4
                            g = c // 2
                            for i, d in enumerate(ds):
                                sl = (T + d) % 4
                                S.emit("pe", lambda e, bk=bk, c4=c4, c=c, g=g, d=d, sl=sl, i=i, T=T: e.matmul(
                                    bk[:, c4 * 128:(c4 + 1) * 128], lhsT=hnr[:, sl, c * 128:(c + 1) * 128],
                                    rhs=blk(g, T, d), start=(i == 0), stop=(i == len(ds) - 1)),
                                    reads=[hnrb[sl], PBb], writes=[bb], inc=(c4 == 3 and i == len(ds) - 1))
                        S.emit("act", lambda e, bk=bk, hb=hb, dd=dd: e.activation(
                            out=dT[:, dd, hb * 4:(hb + 1) * 4, :].rearrange("p c t -> p (c t)"), in_=bk[:, :], func=AF.Identity),
                            reads=[bb], writes=[dTb[dd]])
                    for half in range(2):
                        bk, bb = psum()
                        for gg in range(2):
                            g = half * 2 + gg
                            for kc in range(2):
                                c = 2 * g + kc
                                S.emit("pe", lambda e, bk=bk, gg=gg, g=g, kc=kc, c=c, dd=dd: e.matmul(
                                    bk[:, gg * 256:(gg + 1) * 256], lhsT=dT[:, dd, c, :], rhs=Wp[:, g, kc, :],
                                    start=(kc == 0), stop=(kc == 1)), reads=[dTb[dd], Wpb], writes=[bb],
                                    inc=(gg == 1 and kc == 1))
                        S.emit("dve", lambda e, bk=bk, T=T, half=half: e.tensor_tensor(
                            out=X[:, T, half * 512:(half + 1) * 512], in0=bk[:, :],
                            in1=X[:, T, half * 512:(half + 1) * 512], op=ALU.add),
                            reads=[bb, Xb[T]], writes=[Xb[T]])

                pending_out = list(range(ntile))
                done_src = []
                batches = [srcs[i:i + 4] for i in range(0, len(srcs), 4)]

                def pstats(sg_):
                    aps_ = [src_of(s) for s in sg_]
                    return aps_, norm_stats(sg_, lambda j, aps_=aps_: aps_[j][0], [a[1] for a in aps_])

                nxt_stats = pstats(batches[0])
                for bi, sg in enumerate(batches):
                    aps, k = nxt_stats
                    for j, s in enumerate(sg):
                        if j == min(2, len(sg) - 1) and bi + 1 < len(batches):
                            nxt_stats = pstats(batches[bi + 1])
                        sl = s % 4
                        S.emit("dve", lambda e, j=j, sl=sl, aps=aps, k=k: e.tensor_scalar(
                            out=hnr[:, sl, :], in0=aps[j][0], scalar1=rstd[:, k, j:j + 1], scalar2=None, op0=ALU.mult),
                            reads=[aps[j][1], statb[k]], writes=[hnrb[sl]])
                        done_src.append(s)
                        while pending_out:
                            T = pending_out[0]
                            need = [T + d for d in (-1, 0, 1) if (T + d) in srcs]
                            if all(x in done_src for x in need):
                                do_out(T)
                                pending_out.pop(0)
                            else:
                                break
                assert not pending_out
                prenorm["kvn"] = norm_to_hn(lambda j: MXp[:, j, :], MXpb)
                S.barrier()

        def winattn_phase(seg):
            ntile = seg["ntile"]
            sample = seg["sample"]
            qtiles = seg["core"]
            with phase() as ph:
                hTt = sb("hTt", [128, 4, 8, 128], BF16, ph)
                hTtb = [S.buf("hTt%d" % i) for i in range(4)]
                tA = sb("tA", [128, 8, 64], F32, ph)
                tB = sb("tB", [128, 8, 64], F32, ph)
                tAb = S.buf("tA")
                tBb = S.buf("tB")
                qr = sb("qr", [128, 16, 64], BF16, ph)
                qrb = S.buf("qr")
                kr = sb("kr", [128, 4, 64], BF16, ph)
                krb = S.buf("kr")
                qT = sb("qTw", [64, 16, 128], BF16, ph)
                qTb = S.buf("qTw")
                KT = sb("KT", [64, 4, 4, 128], BF16, ph)
                KTb = [S.buf("KT%d" % i) for i in range(4)]
                VA = sb("VA", [128, 4, 4, 128], BF16, ph)
                VAb = [S.buf("VA%d" % i) for i in range(4)]
                NPT = 6
                PT = sb("PTw", [128, NPT, 512], BF16, ph)
                PTb = [S.buf("PTw%d" % i) for i in range(NPT)]
                rden = sb("rdenw", [64, 2, 512], F32, ph)
                rdenb = [S.buf("rdenw0"), S.buf("rdenw1")]
                oT = sb("oTw", [128, 8, 128], BF16, ph)
                oTb = S.buf("oTw")
                RT = sb("RT", [128, 18, 64], F32, ph)
                RTb = S.buf("RT")
                nrt = 18 if sample else 16
                S.dma("sp", RT[:, 0:nrt, :].rearrange("p a b -> p (a b)"), (ropeS_d if sample else ropeP_d), RTb, True)
                for sl in range(4):
                    S.emit("dve", lambda e, sl=sl: e.memset(VA[:, sl, :, 64:128], 1.0), writes=[VAb[sl]])
                wqkv3 = prenorm.pop("wqkv3")
                wo3 = [W.get(("ao", p)) for p in range(2)]
                wqkv = [(a, b) for (a, b, c) in wqkv3]
                wo = [(a, b) for (a, b, c) in wo3]
                sinkR = sb("sinkR", [1, 16, 128], BF16, ph)
                S.emit("dve", lambda e: e.tensor_copy(out=sinkR[0:1, :, :],
                                                      in_=sinkE[0:1, :].unsqueeze(2).broadcast_to([1, 16, 128])),
                       reads=[sinkb], writes=[sinkb])
                pt_rr = [0]
                rd_rr = [0]

                def rope(src3, tab_t, dst3, nh, srcb, dstb):
                    cosb = RT[:, tab_t, 0:32].unsqueeze(1).broadcast_to([128, nh, 32])
                    sinb = RT[:, tab_t, 32:64].unsqueeze(1).broadcast_to([128, nh, 32])
                    x1 = src3[:, :, 0:32]
                    x2 = src3[:, :, 32:64]
                    A = tA[:, 0:nh, :]
                    B = tB[:, 0:nh, :]
                    S.emit("dve", lambda e: e.tensor_tensor(out=A[:, :, 0:32], in0=x1, in1=cosb, op=ALU.mult),
                           reads=[srcb, RTb], writes=[tAb])
                    S.emit("dve", lambda e: e.tensor_tensor(out=A[:, :, 32:64], in0=x2, in1=cosb, op=ALU.mult),
                           reads=[srcb, RTb], writes=[tAb])
                    S.emit("dve", lambda e: e.tensor_tensor(out=B[:, :, 0:32], in0=x2, in1=sinb, op=ALU.mult),
                           reads=[srcb, RTb], writes=[tBb])
                    S.emit("dve", lambda e: e.tensor_tensor(out=B[:, :, 32:64], in0=x1, in1=sinb, op=ALU.mult),
                           reads=[srcb, RTb], writes=[tBb])
                    S.emit("dve", lambda e: e.tensor_tensor(out=dst3[:, :, 0:32], in0=A[:, :, 0:32], in1=B[:, :, 0:32],
                                                            op=ALU.subtract), reads=[tAb, tBb], writes=[dstb])
                    S.emit("dve", lambda e: e.tensor_tensor(out=dst3[:, :, 32:64], in0=A[:, :, 32:64], in1=B[:, :, 32:64],
                                                            op=ALU.add), reads=[tAb, tBb], writes=[dstb])


                pre_na = prenorm.pop("win_na", set())

                def Na(t):
                    if t in pre_na:
                        pre_na.discard(t)
                        return
                    sl = t % 4
                    k = norm_stats([t], lambda j: X[:, t, :], [Xb[t]])
                    S.emit("dve", lambda e: e.tensor_scalar(out=hn[:, sl, :], in0=X[:, t, :], scalar1=rstd[:, k, 0:1],
                                                            scalar2=None, op0=ALU.mult),
                           reads=[Xb[t], statb[k]], writes=[hnb[sl]])

                def Nb(t):
                    sl = t % 4
                    bk, bb = psum()
                    bv = bk[:].bitcast(BF16)
                    for c in range(8):
                        S.emit("pe", lambda e, c=c: e.transpose(
                            out=bv[:, c * 128:(c + 1) * 128], in_=hn[:, sl, c * 128:(c + 1) * 128], identity=ident[:]),
                            reads=[hnb[sl], identb], writes=[bb], inc=(c == 7))
                    S.emit("act", lambda e: e.activation(
                        out=hTt[:, sl, :, :].rearrange("p c t -> p (c t)"), in_=bv[:, :], func=AF.Identity),
                        reads=[bb], writes=[hTtb[sl]])

                def proj(t, nb):
                    w, wb = wqkv[nb]
                    hs = t % 4
                    bk, bb = psum()
                    for kc in range(8):
                        S.emit("pe", lambda e, bk=bk, kc=kc, w=w: e.matmul(
                            bk[:, :], lhsT=hTt[:, hs, kc, :], rhs=w[:, kc, :],
                            start=(kc == 0), stop=(kc == 7)), reads=[hTtb[hs], wb], writes=[bb], inc=(kc == 7))
                    return bk, bb

                def KVstage(t):
                    sl = t % 4
                    bk2, bb2 = proj(t, 2)
                    S.emit("act", lambda e: e.activation(
                        out=VA[:, sl, :, 0:64], in_=bk2[:, 256:512].rearrange("p (h d) -> p h d", d=64), func=AF.Identity),
                        reads=[bb2], writes=[VAb[sl]])
                    rope(bk2[:, 0:256].rearrange("p (h d) -> p h d", d=64), t, kr[:, :, :], 4, bb2, krb)

                def TKstage(t):
                    sl = t % 4
                    bk, bb = psum()
                    bv = bk[:].bitcast(BF16)
                    for kv in range(4):
                        S.emit("pe", lambda e, kv=kv: e.transpose(
                            out=bv[0:64, kv * 128:(kv + 1) * 128], in_=kr[:, kv, :], identity=ident[:]),
                            reads=[krb, identb], writes=[bb], inc=(kv == 3))
                    S.emit("act", lambda e: e.activation(
                        out=KT[:, sl, :, :].rearrange("p a b -> p (a b)"), in_=bv[0:64, 0:512], func=AF.Identity),
                        reads=[bb], writes=[KTb[sl]])

                def Qstage(t, nb):
                    bkq, bbq = proj(t, nb)
                    rope(bkq[:, :].rearrange("p (h d) -> p h d", d=64), t, qr[:, nb * 8:(nb + 1) * 8, :], 8,
                         bbq, qrb)

                def TQstage(t):
                    for nb in range(2):
                        bk, bb = psum()
                        bv = bk[:].bitcast(BF16)
                        for hh in range(8):
                            S.emit("pe", lambda e, hh=hh, nb=nb, bv=bv: e.transpose(
                                out=bv[0:64, hh * 128:(hh + 1) * 128], in_=qr[:, nb * 8 + hh, :], identity=ident[:]),
                                reads=[qrb, identb], writes=[bb], inc=(hh == 7))
                        if nb == 0:
                            S.emit("act", lambda e, bv=bv, nb=nb: e.activation(
                                out=qT[:, nb * 8:(nb + 1) * 8, :].rearrange("p a b -> p (a b)"),
                                in_=bv[0:64, 0:1024], func=AF.Identity), reads=[bb], writes=[qTb])
                        else:
                            S.emit("dve", lambda e, bv=bv, nb=nb: e.tensor_copy(
                                out=qT[:, nb * 8:(nb + 1) * 8, :].rearrange("p a b -> p (a b)"),
                                in_=bv[0:64, 0:1024]), reads=[bb], writes=[qTb])

                def attn(j, hook1, hook2, hook3):
                    keys = []
                    if j - 1 >= 0:
                        mi = 0
                        if sample and j == qtiles[0]:
                            mi = 2
                        keys.append((j - 1, mi))
                    keys.append((j, None))
                    if j + 1 < ntile:
                        mi = 1
                        if sample and j == qtiles[-1]:
                            mi = 3
                        keys.append((j + 1, mi))
                    stA = {}

                    def stageA(kv):
                        pts = []
                        for (t, mi) in keys:
                            sl = t % 4
                            bk, bb = psum()
                            S.emit("pe", lambda e, bk=bk, sl=sl, kv=kv: e.matmul(
                                bk[:, :], lhsT=KT[:, sl, kv, :],
                                rhs=qT[:, 4 * kv:4 * kv + 4, :].rearrange("p a b -> p (a b)"), start=True, stop=True),
                                reads=[KTb[sl], qTb], writes=[bb], inc=True)
                            pi = pt_rr[0]
                            pt_rr[0] = (pi + 1) % NPT
                            S.emit("act", lambda e, bk=bk, pi=pi: e.activation(
                                out=PT[:, pi, :], in_=bk[:, :], func=AF.Exp, scale=0.125), reads=[bb], writes=[PTb[pi]])
                            if mi is not None:
                                S.emit("dve", lambda e, pi=pi, mi=mi: e.tensor_tensor(
                                    out=PT[:, pi, :].rearrange("p (h q) -> p h q", h=4),
                                    in0=PT[:, pi, :].rearrange("p (h q) -> p h q", h=4),
                                    in1=masks[:, mi, :].unsqueeze(1).broadcast_to([128, 4, 128]), op=ALU.mult),
                                    reads=[PTb[pi], masksb], writes=[PTb[pi]])
                            pts.append((pi, sl))
                        stA[kv] = pts

                    def stageB(kv):
                        pts = stA[kv]
                        bk, bb = psum()
                        for i, (pi, sl) in enumerate(pts):
                            last = i == len(pts) - 1
                            S.emit("pe", lambda e, bk=bk, pi=pi, sl=sl, i=i, last=last: e.matmul(
                                bk[:, :], lhsT=VA[:, sl, kv, :], rhs=PT[:, pi, :], start=(i == 0), stop=last),
                                reads=[VAb[sl], PTb[pi]], writes=[bb], inc=last)
                        ri = rd_rr[0]
                        rd_rr[0] ^= 1
                        for hl in range(4):
                            S.emit("act", lambda e, bk=bk, ri=ri, hl=hl: e.activation(
                                out=rden[0:64, ri, hl * 128:(hl + 1) * 128], in_=bk[64:128, hl * 128:(hl + 1) * 128],
                                func=AF.Ln, bias=sinkEb[64:128, 4 * kv + hl:4 * kv + hl + 1]),
                                reads=[bb, sinkEbb], writes=[rdenb[ri]])
                        S.emit("act", lambda e, ri=ri: e.activation(
                            out=rden[0:64, ri, :], in_=rden[0:64, ri, :], func=AF.Exp, scale=-1.0),
                            reads=[rdenb[ri]], writes=[rdenb[ri]])
                        for par in range(2):
                            S.emit("dve", lambda e, bk=bk, ri=ri, par=par: e.tensor_tensor(
                                out=oT[par * 64:(par + 1) * 64, 2 * kv:2 * kv + 2, :],
                                in0=bk[0:64, :].rearrange("p (i two q) -> p i two q", two=2, q=128)[:, :, par, :],
                                in1=rden[0:64, ri, :].rearrange("p (i two q) -> p i two q", two=2, q=128)[:, :, par, :],
                                op=ALU.mult), reads=[bb, rdenb[ri]], writes=[oTb])

                    stageA(0)
                    stageA(1)
                    stageB(0)
                    stageA(2)
                    hook1()
                    stageB(1)
                    stageA(3)
                    hook2()
                    stageB(2)
                    stageB(3)
                    hook3()
                    return oproj

                def oproj(j):
                    for half in range(2):
                        w, wb = wo[half]
                        bk, bb = psum()
                        for c in range(8):
                            S.emit("pe", lambda e, bk=bk, c=c, w=w: e.matmul(
                                bk[:, :], lhsT=oT[:, c, :], rhs=w[:, c, :], start=(c == 0), stop=(c == 7)),
                                reads=[oTb, wb], writes=[bb], inc=(c == 7))
                        S.emit("dve", lambda e, bk=bk, half=half: e.tensor_tensor(
                            out=X[:, j, half * 512:(half + 1) * 512], in0=bk[:, :],
                            in1=X[:, j, half * 512:(half + 1) * 512], op=ALU.add),
                            reads=[bb, Xb[j]], writes=[Xb[j]])


                def valid(t):
                    return 0 <= t < ntile

                if valid(0):
                    Na(0)
                pending = None
                for j in range(-3, ntile):
                    if valid(j + 3):
                        Nb(j + 3)
                    if valid(j + 2):
                        KVstage(j + 2)
                    if pending is not None:
                        oproj(pending)
                        pending = None

                    def hook1(j=j):
                        if valid(j + 4):
                            Na(j + 4)
                        if valid(j + 2):
                            TKstage(j + 2)
                        if (j + 1) in qtiles:
                            Qstage(j + 1, 0)

                    def hook2(j=j):
                        if (j + 1) in qtiles:
                            Qstage(j + 1, 1)

                    def hook3(j=j):
                        if (j + 1) in qtiles:
                            TQstage(j + 1)
                    if j in qtiles:
                        attn(j, hook1, hook2, hook3)
                        pending = j
                    else:
                        hook1()
                        hook2()
                        hook3()
                if pending is not None:
                    oproj(pending)
                for (a, b, c) in wqkv3 + wo3:
                    W.release(c)
                S.barrier()

        def final_phase(seg, nxt):
            tiles = seg["core"]
            loaded = set()
            with phase() as ph:
                NY = 4
                ystage = sb("ystage", [128, NY, D], F32, ph)
                ystageb = [S.buf("ystage%d" % i) for i in range(NY)]
                evs = []
                yi = 0
                for tl in groups_of(tiles):
                    f, bs = xsrc(tl)
                    k = norm_stats(tl, f, bs)
                    for j, t in enumerate(tl):
                        S.emit("dve", lambda e, j=j, t=t, k=k, yi=yi: e.scalar_tensor_tensor(
                            out=ystage[:, yi, :], in0=X[:, t, :], scalar=rstd[:, k, j:j + 1], in1=gfin[:, :],
                            op0=ALU.mult, op1=ALU.mult), reads=[Xb[t], statb[k], gfinb], writes=[ystageb[yi]])
                        evs.append(S.dma("sp", seg["out"](t), ystage[:, yi, :], ystageb[yi], False))
                        yi = (yi + 1) % NY
                        if nxt is not None and t < nxt["ntile"]:
                            S.dma("sp", X[:, t, :], nxt["xin"](t), Xb[t], True)
                            loaded.add(t)
                if nxt is not None:
                    for t in range(nxt["ntile"]):
                        if t not in loaded:
                            S.dma("sp", X[:, t, :], nxt["xin"](t), Xb[t], True)
                for b in ystageb:
                    for e_ in ("act", "dve", "sp", "pe"):
                        S._wait(e_, list(b.r.items()))

        def dump_phase(seg):
            for t in seg["core"]:
                S.dma("sp", seg["out"](t), X[:, t, :], Xb[t], False)

        segs = []
        for b in range(2):
            segs.append(dict(sample=False, ntile=16, core=list(range(16)),
                             xin=(lambda t, b=b: xp[b, t * 128:(t + 1) * 128, :]),
                             out=(lambda t, b=b: yp[b, t * 128:(t + 1) * 128, :]),
                             mem=memp[b]))
        segs.append(dict(sample=True, ntile=18, core=list(range(1, 17)),
                         xin=(lambda t: xs[(t + 1) * 128:(t + 2) * 128, :]),
                         out=(lambda t: ys[(t - 1) * 128:t * 128, :]),
                         mem=mems))

        seglist = [segs[i] for i in segs_sel]
        for t in range(seglist[0]["ntile"]):
            S.dma("sp", X[:, t, :], seglist[0]["xin"](t), Xb[t], True)
        for si, seg in enumerate(seglist):
            nxt = seglist[si + 1] if si + 1 < len(seglist) else None
            alltiles = list(range(seg["ntile"]))
            stages = [
                ("pool", lambda: pool_phase(seg)),
                ("kv0", lambda: memkv_phase(seg["mem"], 0, alltiles[0:4])),
                ("xa0", lambda: memattn_phase(alltiles, 0)),
                ("mlp0", lambda: mlp_phase(alltiles, 0)),
                ("win", lambda: winattn_phase(seg)),
                ("kv1", lambda: memkv_phase(seg["mem"], 1, seg["core"][0:4])),
                ("xa1", lambda: memattn_phase(seg["core"], 1)),
                ("mlp1", lambda: mlp_phase(seg["core"], 1, seg if stop_after is None else None, nxt)),
            ]
            stopped = False
            for name, fn in stages:
                fn()
                if stop_after == name:
                    stopped = True
                    break
            if stopped:
                dump_phase(seg)
                if nxt is not None:
                    for t in range(nxt["ntile"]):
                        S.dma("sp", X[:, t, :], nxt["xin"](t), Xb[t], True)
            elif stop_after is not None:
                final_phase(seg, nxt)
            S.barrier()
        S.finish()
        blk = st.enter_context(nc.Block())
        S.replay(blk)
    return nc


POOL_WINDOWS = (2, 4, 8, 16)


def _pool_matrix(Sn, w):
    t = np.arange(Sn)
    lo = np.clip(t - w // 2, 0, Sn)
    hi = np.clip(t + w // 2, 0, Sn)
    cnt = (hi - lo).astype(np.float64)
    j = np.arange(Sn)[:, None]
    M = ((j >= lo[None, :]) & (j < hi[None, :])).astype(np.float64) / cnt[None, :]
    M -= np.eye(Sn)
    return M.astype(np.float32)


def _pool_blocks(first_edge, last_edge):
    out = np.zeros((128, NPB, 128), np.float32)
    for g, w in enumerate(POOL_WINDOWS):
        M = _pool_matrix(5 * 128, w)

        def B(rt, ct):
            return M[rt * 128:(rt + 1) * 128, ct * 128:(ct + 1) * 128]
        out[:, g * 7 + 0] = B(1, 2)
        out[:, g * 7 + 1] = B(2, 2)
        out[:, g * 7 + 2] = B(3, 2)
        out[:, g * 7 + 3] = B(0, 0)
        out[:, g * 7 + 4] = B(4, 4)
        out[:, g * 7 + 5] = B(0, 0) if first_edge else B(2, 2)
        out[:, g * 7 + 6] = B(4, 4) if last_edge else B(2, 2)
    return out.reshape(128, NPB * 128)


def _masks(first_edge, last_edge):
    kk = np.arange(128)[:, None]
    qq = np.arange(128)[None, :]
    prev = (kk >= qq).astype(np.float32)
    nxt = (kk <= qq).astype(np.float32)
    m = np.zeros((128, 4, 128), np.float32)
    m[:, 0] = prev
    m[:, 1] = nxt
    m[:, 2] = 0.0 if first_edge else prev
    m[:, 3] = 0.0 if last_edge else nxt
    return m.reshape(128, 4 * 128)


def _rope_table(pos0, ntile):
    inv = (np.float32(10000.0) ** (-(np.arange(0, 64, 2, dtype=np.float32) / np.float32(64.0)))).astype(np.float32)
    pos = (pos0 + np.arange(ntile * 128)).astype(np.float32)
    ang = (pos[:, None] * inv[None, :]).astype(np.float32)
    tab = np.concatenate([np.cos(ang), np.sin(ang)], axis=1).astype(np.float32)
    tab = tab.reshape(ntile, 128, 64).transpose(1, 0, 2)
    return np.ascontiguousarray(tab).reshape(128, ntile * 64)


_NC_CACHE = {}


def _get_nc(stop_after=None):
    if stop_after not in _NC_CACHE:
        _NC_CACHE[stop_after] = build_program(stop_after)
    return _NC_CACHE[stop_after]


def make_in_maps(x_prompt, x_sample, mem_prompt, mem_sample, norm_mix, pool_w, pool_scale,
                 attn_qkv, attn_o, attn_sink, norm_x, norm_mem, x_wq, x_wkv, x_wo,
                 norm_mlp, w_up, w_down, norm_final):
    f = lambda a: np.ascontiguousarray(np.asarray(a, dtype=np.float32))
    x_prompt, x_sample, mem_prompt, mem_sample = f(x_prompt), f(x_sample), f(mem_prompt), f(mem_sample)
    vecs = [f(norm_mix)[0], f(norm_mix)[1], f(norm_x)[0], f(norm_x)[1], f(norm_mem)[0], f(norm_mem)[1],
            f(norm_mlp)[0], f(norm_mlp)[1], np.zeros(D, np.float32)]
    gfm = np.stack([v.reshape(8, 128).T for v in vecs], axis=1).reshape(128, 72)
    shared = {
        "pool_w": f(pool_w).reshape(1024, 256),
        "pool_scale": f(pool_scale).reshape(1, D),
        "attn_qkv": f(attn_qkv).reshape(D, 1536),
        "attn_o": f(attn_o).reshape(D, D),
        "attn_sink": f(attn_sink).reshape(1, 16),
        "x_wq": f(x_wq).reshape(2 * D, D),
        "x_wkv": f(x_wkv).reshape(2 * D, 2 * D),
        "x_wo": f(x_wo).reshape(2 * D, D),
        "w_up": f(w_up).reshape(2 * D, 4 * D),
        "w_down": f(w_down).reshape(8 * D, D),
        "gfm": np.ascontiguousarray(gfm),
        "gfinal": f(norm_final).reshape(1, D),
        "ident": np.eye(128, dtype=np.float32),
        "ropeP": _rope_table(0, 16),
    }
    in_maps = []
    for c in range(NCORES):
        sb_, ch = c // 4, c % 4
        start = ch * 2048
        xs_ = np.zeros((2560, D), np.float32)
        lo, hi = start - 256, start + 2048 + 256
        slo, shi = max(lo, 0), min(hi, DEC_SEQ)
        xs_[slo - lo:shi - lo] = x_sample[sb_, slo:shi]
        m = dict(shared)
        m["xp"] = np.ascontiguousarray(x_prompt[2 * c:2 * c + 2])
        m["xs"] = xs_
        m["memp"] = np.ascontiguousarray(mem_prompt[2 * c:2 * c + 2])
        m["mems"] = np.ascontiguousarray(mem_sample[sb_])
        m["pbc"] = _pool_blocks(ch == 0, ch == 3)
        m["masks"] = _masks(ch == 0, ch == 3)
        m["ropeS"] = _rope_table(start - 128, 18)
        in_maps.append(m)
    return in_maps


def kernel(**inputs):
    in_maps = make_in_maps(**inputs)
    nc = _get_nc(None)
    res = run_bass_kernel_spmd(nc, in_maps, core_ids=list(range(NCORES)))
    y_prompt = np.empty((16, SEQ, D), np.float32)
    y_sample = np.empty((2, DEC_SEQ, D), np.float32)
    for c in range(NCORES):
        r = res.results[c]
        y_prompt[2 * c:2 * c + 2] = r["yp"]
        sb_, ch = c // 4, c % 4
        y_sample[sb_, ch * 2048:(ch + 1) * 2048] = r["ys"]
    return (y_prompt, y_sample)
```
